# Optimizing a Trainium2 kernel written in Bass

```python
import math
import jax, jax.numpy as jnp
from jax import lax
import numpy as np

D_MODEL = 1024
BATCH = 8
SEQ = 2048
DEPTH = 2
DEC_BATCH = 128
DEC_SEQ = 4
PAST_LEN = 16384
PAGE_SIZE = 128

W_A = D_MODEL
CONV_A = 31
HG_HEADS = 8
HG_DK = 128
HG_DV = D_MODEL // HG_HEADS
W_BK = HG_HEADS * HG_DK
W_BV = HG_HEADS * HG_DV
HG_CHUNK = 64
W_C = D_MODEL
CONV_C = 3
N_BRANCH = 3
EPS = 1e-6

kernel_name = "hybrid_conformer_hgrn2_shortconv_step"


def _col_sizes():
    return [W_A, W_A, W_A, W_BK, W_BK, W_BV, W_BV, W_C, W_C, W_C, W_C, N_BRANCH * D_MODEL]


def _split_cols(p):
    idx = np.cumsum(_col_sizes())[:-1].tolist()
    return jnp.split(p, idx, axis=-1)


def rmsnorm(x, g):
    xf = x.astype(jnp.float32)
    y = xf * lax.rsqrt(jnp.mean(xf * xf, axis=-1, keepdims=True) + EPS) * g.astype(jnp.float32)
    return y.astype(x.dtype)


def layernorm(x, g, b):
    xf = x.astype(jnp.float32)
    mu = jnp.mean(xf, axis=-1, keepdims=True)
    xc = xf - mu
    var = jnp.mean(xc * xc, axis=-1, keepdims=True)
    y = xc * lax.rsqrt(var + EPS) * g.astype(jnp.float32) + b.astype(jnp.float32)
    return y.astype(x.dtype)


def causal_dwconv(x, prev, w, b=None):
    k, c = w.shape
    xp = jnp.concatenate([prev.astype(x.dtype), x], axis=1)
    y = lax.conv_general_dilated(xp, w.astype(x.dtype)[:, None, :], window_strides=(1,), padding='VALID',
                                 dimension_numbers=('NWC', 'WIO', 'NWC'), feature_group_count=c)
    if b is not None:
        y = y + b.astype(x.dtype)
    return y, xp[:, -(k - 1):]


def hgrn2_recurrence(q, logf, k, v, s0):
    bsz, t = q.shape[0], q.shape[1]
    c = math.gcd(t, HG_CHUNK)
    n = t // c

    def to_chunks(a):
        return a.astype(jnp.float32).reshape(bsz, n, c, a.shape[2], a.shape[3]).transpose(1, 0, 3, 2, 4)

    qc, lc, kc, vc = to_chunks(q), to_chunks(logf), to_chunks(k), to_chunks(v)
    mask = jnp.tril(jnp.ones((c, c), dtype=bool))[:, :, None]

    def step(s, inp):
        qi, li, ki, vi = inp
        bcum = jnp.cumsum(li, axis=2)
        inter = jnp.einsum('bhck,bhkv->bhcv', qi * jnp.exp(bcum), s)
        diff = bcum[:, :, :, None, :] - bcum[:, :, None, :, :]
        decay = jnp.exp(jnp.where(mask, diff, -jnp.inf))
        att = jnp.einsum('bhtsk,bhsk->bhts', qi[:, :, :, None, :] * decay, ki)
        o = inter + jnp.einsum('bhts,bhsv->bhtv', att, vi)
        blast = bcum[:, :, -1]
        s_new = jnp.exp(blast)[..., None] * s + jnp.einsum(
            'bhsk,bhsv->bhkv', ki * jnp.exp(blast[:, :, None, :] - bcum), vi)
        return s_new, o

    s_fin, o = lax.scan(step, s0.astype(jnp.float32), (qc, lc, kc, vc))
    o = o.transpose(1, 0, 3, 2, 4).reshape(bsz, t, q.shape[2], v.shape[3])
    return o, s_fin


def mixer_layer(x, prev_a, s0, prev_c, norm_g, w_in, gate_b, conv_a_w, conv_a_b, ln_g, ln_b, w_a_out,
                lb, hg_norm_g, w_b_out, conv_c_w, w_c_out, w_o):
    bsz, t, _ = x.shape
    h = rmsnorm(x, norm_g)
    proj = jnp.einsum('btd,dn->btn', h, w_in.astype(h.dtype))
    a_val, a_gate, z_a, q, f_pre, i_v, z_b, b_g, c_g, h_c, z_c, g_pre = _split_cols(proj)

    u = a_val * jax.nn.sigmoid(a_gate)
    ca, new_a = causal_dwconv(u, prev_a, conv_a_w, conv_a_b)
    ya = jax.nn.silu(layernorm(ca, ln_g, ln_b)) * jax.nn.silu(z_a)
    ya = ya @ w_a_out.astype(ya.dtype)

    q = q.reshape(bsz, t, HG_HEADS, HG_DK)
    f_pre = f_pre.reshape(bsz, t, HG_HEADS, HG_DK).astype(jnp.float32)
    i_v = i_v.reshape(bsz, t, HG_HEADS, HG_DV)
    lbh = lb.reshape(HG_HEADS, HG_DK)
    logf = jnp.logaddexp(jnp.log(lbh), jnp.log1p(-lbh) + jax.nn.log_sigmoid(f_pre))
    k = -jnp.expm1(logf)
    o, s_new = hgrn2_recurrence(q, logf, k, i_v, s0)
    o = o * lax.rsqrt(jnp.mean(o * o, axis=-1, keepdims=True) + EPS)
    o = (o.reshape(bsz, t, W_BV) * hg_norm_g.astype(jnp.float32)).astype(x.dtype)
    yb = (o * jax.nn.silu(z_b)) @ w_b_out.astype(x.dtype)

    cc, new_c = causal_dwconv(c_g * h_c, prev_c, conv_c_w)
    yc = (b_g * cc * jax.nn.silu(z_c)) @ w_c_out.astype(x.dtype)

    ga, gb, gc = jnp.split(jax.nn.sigmoid(g_pre + gate_b.astype(g_pre.dtype)), N_BRANCH, axis=-1)
    m = ga * ya + gb * yb + gc * yc
    x = x + m @ w_o.astype(m.dtype)
    return x, new_a, s_new.astype(x.dtype), new_c


def setup_inputs(seed: int = 0) -> dict:
    key = jax.random.key(seed)
    ks = jax.random.split(key, 20)
    n_cols = sum(_col_sizes())
    f32 = jnp.float32
    nrm = lambda k, shape, s: (jax.random.normal(k, shape, f32) * s)
    return {
        "x_prompt": nrm(ks[0], (BATCH, SEQ, D_MODEL), 1.0),
        "x_sample": nrm(ks[1], (DEC_BATCH, DEC_SEQ, D_MODEL), 1.0),
        "state_conv_a": nrm(ks[2], (DEPTH, DEC_BATCH, CONV_A - 1, W_A), 1.0),
        "state_hgrn": nrm(ks[3], (DEPTH, DEC_BATCH, HG_HEADS, HG_DK, HG_DV), 0.5),
        "state_conv_c": nrm(ks[4], (DEPTH, DEC_BATCH, CONV_C - 1, W_C), 1.0),
        "norm_g": 1.0 + nrm(ks[5], (DEPTH, D_MODEL), 0.01),
        "w_in": nrm(ks[6], (DEPTH, D_MODEL, n_cols), D_MODEL ** -0.5),
        "gate_b": nrm(ks[7], (DEPTH, N_BRANCH * D_MODEL), 0.01),
        "conv_a_w": nrm(ks[8], (DEPTH, CONV_A, W_A), CONV_A ** -0.5),
        "conv_a_b": nrm(ks[9], (DEPTH, W_A), 0.01),
        "ln_g": 1.0 + nrm(ks[10], (DEPTH, W_A), 0.01),
        "ln_b": nrm(ks[11], (DEPTH, W_A), 0.01),
        "w_a_out": nrm(ks[12], (DEPTH, W_A, D_MODEL), W_A ** -0.5),
        "lb_raw": nrm(ks[13], (DEPTH, W_BK), 0.5),
        "hg_norm_g": 1.0 + nrm(ks[14], (DEPTH, W_BV), 0.01),
        "w_b_out": nrm(ks[15], (DEPTH, W_BV, D_MODEL), W_BV ** -0.5),
        "conv_c_w": nrm(ks[16], (DEPTH, CONV_C, W_C), CONV_C ** -0.5),
        "w_c_out": nrm(ks[17], (DEPTH, W_C, D_MODEL), W_C ** -0.5),
        "w_o": nrm(ks[18], (DEPTH, D_MODEL, D_MODEL), D_MODEL ** -0.5),
        "final_norm_g": 1.0 + nrm(ks[19], (D_MODEL,), 0.01),
    }


def reference(x_prompt, x_sample, state_conv_a, state_hgrn, state_conv_c, norm_g, w_in, gate_b, conv_a_w,
              conv_a_b, ln_g, ln_b, w_a_out, lb_raw, hg_norm_g, w_b_out, conv_c_w, w_c_out, w_o, final_norm_g):
    lb_all = jnp.cumsum(jax.nn.softmax(lb_raw.astype(jnp.float32), axis=0), axis=0)
    lb_all = lb_all - lb_all[0:1]

    xp, xs = x_prompt, x_sample
    pa, ph, pc, sa, sh, sc = [], [], [], [], [], []
    for l in range(DEPTH):
        w = (norm_g[l], w_in[l], gate_b[l], conv_a_w[l], conv_a_b[l], ln_g[l], ln_b[l], w_a_out[l],
             lb_all[l], hg_norm_g[l], w_b_out[l], conv_c_w[l], w_c_out[l], w_o[l])
        zero_a = jnp.zeros((BATCH, CONV_A - 1, W_A), xp.dtype)
        zero_h = jnp.zeros((BATCH, HG_HEADS, HG_DK, HG_DV), jnp.float32)
        zero_c = jnp.zeros((BATCH, CONV_C - 1, W_C), xp.dtype)
        xp, na, nh, nc = mixer_layer(xp, zero_a, zero_h, zero_c, *w)
        pa.append(na); ph.append(nh); pc.append(nc)
        xs, na, nh, nc = mixer_layer(xs, state_conv_a[l], state_hgrn[l], state_conv_c[l], *w)
        sa.append(na); sh.append(nh); sc.append(nc)

    y_prompt = rmsnorm(xp, final_norm_g)
    y_sample = rmsnorm(xs, final_norm_g)
    return (y_prompt, y_sample, jnp.stack(pa), jnp.stack(ph), jnp.stack(pc),
            jnp.stack(sa), jnp.stack(sh), jnp.stack(sc))
```

```python
import numpy as np
import concourse.bass as bass
import concourse.mybir as mybir
from concourse.bass_utils import run_bass_kernel_spmd
from contextlib import ExitStack

F32 = mybir.dt.float32
BF16 = mybir.dt.bfloat16
AF = mybir.ActivationFunctionType
ALU = mybir.AluOpType

D = 1024
KC = 8
TPP = 1024
TS = 64
TC = TPP + TS
EPS = 1e-6
BIG = 3.0e38
NL = 256
NLMAX = 512
(C_AVAL, C_AGATE, C_ZA, C_Q, C_F, C_V, C_ZB, C_BG, C_CG, C_HC, C_ZC, C_G) = [i * 1024 for i in range(12)]
V_LB0, V_LB1, V_NG, V_GB, V_CAB, V_LNG, V_LNB, V_HG, V_CCW, V_CAW = 0, 1, 2, 3, 6, 7, 8, 9, 10, 13
NV = 44
K_ID, K_MP, K_MN, K_MS, K_RP, K_RS, K_SM, K_M1, NCON = 0, 128, 256, 384, 448, 960, 1024, 1040, 1168


class Tok:
    __slots__ = ("name", "w", "r")

    def __init__(self, name):
        self.name = name
        self.w = None
        self.r = []


class FW:
    ENGS = ("tensor", "vector", "scalar", "gpsimd", "sync")
    NDMA = 10

    def __init__(self, nc, es):
        self.nc = nc
        self.es = es
        self.eng = {n: getattr(nc, n) for n in self.ENGS}
        self.sem = {}
        self.cnt = {}
        for n in self.ENGS:
            self.sem[n] = es.enter_context(nc.semaphore("s_" + n))
            self.cnt[n] = 0
        self.dnext = {}
        for q in ("sync", "gpsimd"):
            for i in range(self.NDMA):
                k = "d_%s_%d" % (q, i)
                self.sem[k] = es.enter_context(nc.semaphore(k))
                self.cnt[k] = 0
            self.dnext[q] = 0
        self.waited = {}
        self.tk = {}
        self.nwaits = 0
        self.nops = 0
        self.nuniq = 0
        self.bar_tile = es.enter_context(nc.sbuf_tensor("bar_tile", [128, 8], F32))

    def T(self, *key):
        t = self.tk.get(key)
        if t is None:
            t = Tok(str(key))
            self.tk[key] = t
        return t

    def sb(self, name, shape, dt):
        return self.es.enter_context(self.nc.sbuf_tensor(name, list(shape), dt))

    def ps(self, name, shape, dt=F32):
        return self.es.enter_context(self.nc.psum_tensor(name, list(shape), dt))

    def _need(self, en, ev, out):
        if ev is None:
            return
        k, v = ev
        if self.waited.get((en, k), 0) >= v:
            return
        if out.get(k, 0) < v:
            out[k] = v

    def _deps(self, en, reads, writes):
        need = {}
        for t in reads:
            self._need(en, t.w, need)
            if t.name.startswith("('P"):
                for ev in t.r:
                    if ev[0] != en:
                        self._need(en, ev, need)
        for t in writes:
            self._need(en, t.w, need)
            for ev in t.r:
                self._need(en, ev, need)
        return need

    def _emit_waits(self, en, need):
        e = self.eng[en]
        for k, v in need.items():
            e.wait_ge(self.sem[k], v)
            self.waited[(en, k)] = v
            self.nwaits += 1

    def _record(self, ev, reads, writes):
        for t in reads:
            if len(t.r) > 24:
                best = {}
                for k, v in t.r:
                    if best.get(k, 0) < v:
                        best[k] = v
                t.r = list(best.items())
            t.r.append(ev)
        for t in writes:
            t.w = ev
            t.r = []

    def op(self, en, fn, reads=(), writes=()):
        need = self._deps(en, reads, writes)
        if en == "tensor":
            need.pop("tensor", None)
        self._emit_waits(en, need)
        ins = fn(self.eng[en])
        self.cnt[en] += 1
        ins.then_inc(self.sem[en], 1)
        ev = (en, self.cnt[en])
        self._record(ev, reads, writes)
        self.nops += 1
        return ev

    def dma(self, q, out, in_, reads=(), writes=(), **kw):
        i = self.dnext[q]
        self.dnext[q] = (i + 1) % self.NDMA
        k = "d_%s_%d" % (q, i)
        need = self._deps(q, reads, writes)
        if self.cnt[k] > 0:
            self._need(q, (k, self.cnt[k]), need)
        self._emit_waits(q, need)
        ins = self.eng[q].dma_start(out=out, in_=in_, **kw)
        self.cnt[k] += 16
        ins.then_inc(self.sem[k], 16)
        ev = (k, self.cnt[k])
        self._record(ev, reads, writes)
        self.nops += 1
        return ev

    def barrier(self):
        need = {}
        for o in ("tensor", "scalar", "gpsimd"):
            if self.cnt[o] > 0:
                self._need("vector", (o, self.cnt[o]), need)
        for k, v in self.cnt.items():
            if k.startswith("d_sync_") and v > 0:
                self._need("vector", (k, v), need)
        self._emit_waits("vector", need)
        ins = self.eng["vector"].memset(self.bar_tile[:], 0.0)
        self.cnt["vector"] += 1
        ins.then_inc(self.sem["vector"], 1)
        ev = ("vector", self.cnt["vector"])
        for en in ("tensor", "scalar", "gpsimd", "sync"):
            nd = {}
            self._need(en, ev, nd)
            self._emit_waits(en, nd)

    def finish_all(self):
        need = {}
        for k, v in self.cnt.items():
            if v > 0 and k != "sync":
                self._need("sync", (k, v), need)
        self._emit_waits("sync", need)

    def act(self, out, in_, func, reads, writes, scale=1.0, bias=0.0, accum_out=None):
        kw = {}
        if accum_out is not None:
            kw["accum_out"] = accum_out
        return self.op("scalar", lambda e: e.activation(out=out, in_=in_, func=func, scale=scale, bias=bias, **kw),
                       reads, writes)

    def tt(self, out, in0, in1, op, reads, writes, eng="vector"):
        return self.op(eng, lambda e: e.tensor_tensor(out=out, in0=in0, in1=in1, op=op), reads, writes)

    def ts(self, out, in0, s1, op0, reads, writes, s2=None, op1=None, eng="vector"):
        if op1 is None:
            return self.op(eng, lambda e: e.tensor_scalar(out=out, in0=in0, scalar1=s1, scalar2=None, op0=op0),
                           reads, writes)
        return self.op(eng, lambda e: e.tensor_scalar(out=out, in0=in0, scalar1=s1, scalar2=s2, op0=op0, op1=op1),
                       reads, writes)

    def stt(self, out, in0, scalar, in1, op0, op1, reads, writes):
        return self.op("vector", lambda e: e.scalar_tensor_tensor(out=out, in0=in0, scalar=scalar, in1=in1,
                                                                  op0=op0, op1=op1), reads, writes)

    def copy(self, out, in_, reads, writes, eng="vector"):
        return self.op(eng, lambda e: e.tensor_copy(out=out, in_=in_), reads, writes)


class StopBuild(Exception):
    pass


import os as _os
_KSTOP = int(_os.environ.get("KSTOP", "0"))
_KSKIP = _os.environ.get("KSKIP", "")
_ck = [0]


def ck(name):
    _ck[0] += 1
    if _KSTOP and _ck[0] >= _KSTOP:
        print("[build] stopping at checkpoint %d (%s)" % (_ck[0], name))
        raise StopBuild()


class Ring:
    def __init__(self, fw, name, shape, dt, n):
        self.bufs = [fw.sb("%s%d" % (name, i), shape, dt) for i in range(n)]
        self.toks = [fw.T(name, i) for i in range(n)]
        self.i = 0

    def next(self):
        b, t = self.bufs[self.i], self.toks[self.i]
        self.i = (self.i + 1) % len(self.bufs)
        return b, t


def unit_specs(l, W):
    w_in, w_a, w_b, w_c, w_o = W["w_in"], W["w_a_out"], W["w_b_out"], W["w_c_out"], W["w_o"]
    s = []
    wi = lambda c0, n: w_in[l, :, c0:c0 + n]
    for cp in range(4):
        s.append(("AGV", [wi(C_AGATE + cp * 256, 256), wi(C_AVAL + cp * 256, 256)]))
    for h in range(2):
        s.append(("ZA", [wi(C_ZA + h * 512, 512)]))
    for cp in range(4):
        s.append(("AOG", [wi(C_G + cp * 256, 256), w_a[l, :, cp * 256:cp * 256 + 256]]))
    for cp in range(4):
        s.append(("CGH", [wi(C_CG + cp * 256, 256), wi(C_HC + cp * 256, 256)]))
    for cp in range(4):
        s.append(("BGZ", [wi(C_BG + cp * 256, 256), wi(C_ZC + cp * 256, 256)]))
    for cp in range(4):
        s.append(("COG", [wi(C_G + 2048 + cp * 256, 256), w_c[l, :, cp * 256:cp * 256 + 256]]))
    for hg in range(2):
        for hp in range(2):
            h0 = hg * 4 + hp * 2
            s.append(("FQ", [wi(C_F + h0 * 128, 256), wi(C_Q + h0 * 128, 256)]))
        s.append(("V", [wi(C_V + hg * 512, 512)]))
        s.append(("ZB", [wi(C_ZB + hg * 512, 512)]))
    for cp in range(4):
        s.append(("BOG", [wi(C_G + 1024 + cp * 256, 256), w_b[l, :, cp * 256:cp * 256 + 256]]))
    for h in range(2):
        s.append(("WO", [w_o[l, :, h * 512:h * 512 + 512]]))
    return s


def build_program():
    nc = bass.Bass("TRN2", target_bir_lowering=False)

    def din(name, shape):
        return nc.dram_tensor(name, list(shape), F32, kind="ExternalInput").ap()

    def dout(name, shape):
        return nc.dram_tensor(name, list(shape), F32, kind="ExternalOutput").ap()

    xp = din("xp", [2048, D])
    xs = din("xs", [TS, D])
    scaP = din("scaP", [128, 2, 8, 16 * 34])
    scaS = din("scaS", [128, 2, 8, 16 * 30])
    shg = din("shg", [2, 16, 8, 128, 128])
    sccP = din("sccP", [128, 2, 8, 16 * 6])
    vecs = din("vecs", [128, 2 * 8 * NV])
    fngb = din("fngb", [128, D])
    cons = din("cons", [128, NCON])
    W = {"w_in": din("w_in", [2, D, 14336])}
    for n in ("w_a_out", "w_b_out", "w_c_out", "w_o"):
        W[n] = din(n, [2, D, D])
    yp = dout("yp", [2048, D])
    ys = dout("ys", [TS, D])
    napT = dout("napT", [128, 2, 8, 30])
    nhp = dout("nhp", [2, 8, 128, 128])
    ncpT = dout("ncpT", [128, 2, 8, 2])
    nasT = dout("nasT", [128, 2, 8, 16 * 30])
    nhs = dout("nhs", [2, 16, 8, 128, 128])
    ncsT = dout("ncsT", [128, 2, 8, 16 * 2])
    dgscr = nc.dram_tensor("dgscr", [2, 8, 128, 31 * 128], BF16, kind="Internal").ap()

    with ExitStack() as es:
        fw = FW(nc, es)
        T = fw.T
        x_tok = fw.sb("x_tok", [128, 8, D], F32)
        xs_tok = fw.sb("xs_tok", [128, D], F32)
        hT = fw.sb("hT", [128, KC, TC], BF16)
        m = fw.sb("m", [128, KC, TC], BF16)
        wslot = fw.sb("wslot", [128, 3, KC, 512], BF16)
        vec = fw.sb("vec", [128, 2, 8, NV], F32)
        fng = fw.sb("fng", [128, D], F32)
        con = fw.sb("con", [128, NCON], F32)
        ident_b = fw.sb("ident_b", [128, 128], BF16)
        ones_b = fw.sb("ones_b", [128, 128], BF16)
        lbc = fw.sb("lbc", [128, 2, 8], F32)
        utail = fw.sb("utail", [128, 2, 8, 30], BF16)
        S_f = fw.sb("S_f", [128, 2, 8, 128], F32)
        chtail = fw.sb("chtail", [128, 2, 8, 2], F32)
        ss = fw.sb("ss", [128, 16], F32)
        lnv_n = fw.sb("lnv_n", [128, 16], F32)
        rstd_n = fw.sb("rstd_n", [128, 16], F32)
        ident_f = con[:, K_ID:K_ID + 128]
        maskpos = con[:, K_MP:K_MP + 128]
        maskneg = con[:, K_MN:K_MN + 128]
        mask01 = con[:, K_M1:K_M1 + 128]
        maskS = con[0:64, K_MS:K_MS + 64]
        rstP = con[:, K_RP:K_RP + 512]
        rstS = con[:, K_RS:K_RS + 64]
        seqm = con[0:64, K_SM:K_SM + 16]

        PA = fw.ps("PA", [128, 4, 512], F32)
        PT = fw.ps("PT", [128, 2, 512], F32)
        PC = fw.ps("PC", [128, 512], F32)
        PD = fw.ps("PD", [128, 512], F32)
        PTb = PT[:, 0, :].bitcast(BF16)
        pa_i = [0]

        def pa_next():
            i = pa_i[0]
            pa_i[0] = (i + 1) % 4
            return PA[:, i, :], T("PA", i)

        fw.dma("sync", con[:], cons, writes=[T("con")])
        fw.dma("sync", vec[:].rearrange("p a b c -> p (a b c)"), vecs, writes=[T("vec")])
        fw.dma("sync", fng[:], fngb, writes=[T("fng")])
        fw.copy(ident_b[:], ident_f, [T("con")], [T("identb")])
        fw.op("vector", lambda e: e.memset(ones_b[:], 1.0), [], [T("onesb")])
        fw.op("vector", lambda e: e.memset(S_f[:], 0.0), [], [T("S_f", 0), T("S_f", 1)])
        fw.op("vector", lambda e: e.memset(chtail[:], 0.0), [], [T("chtail", 0), T("chtail", 1)])
        fw.op("vector", lambda e: e.memset(ss[:], 1.0), [], [T("ss")])
        fw.op("vector", lambda e: e.memset(PC[:], 0.0), [], [T("PC")])
        fw.op("vector", lambda e: e.memset(lbc[:, 0, :], 0.0), [], [T("lbc")])
        fw.tt(lbc[:, 1, :], vec[:, 1, :, V_LB1], vec[:, 1, :, V_LB0], ALU.subtract, [T("vec")], [T("lbc")])
        fw.act(lbc[:, 1, :], lbc[:, 1, :], AF.Sigmoid, [T("lbc")], [T("lbc")])

        specs = []
        for ps_ in range(2):
            for l in range(2):
                specs += unit_specs(l, W)
        ust = {"next_load": 0, "next_use": 0}

        def load_unit(i):
            kind, segs = specs[i]
            sl = i % 3
            c0 = 0
            for seg in segs:
                n = seg.shape[-1]
                fw.dma("gpsimd", wslot[:, sl, :, c0:c0 + n], seg.rearrange("(kc p) n -> p kc n", p=128),
                       writes=[T("wslot", sl)])
                c0 += n

        def next_unit(kind, ahead=2):
            i = ust["next_use"]
            assert specs[i][0] == kind, (specs[i][0], kind, i)
            while ust["next_load"] < min(len(specs), i + 1 + ahead):
                load_unit(ust["next_load"])
                ust["next_load"] += 1
            ust["next_use"] += 1
            sl = i % 3
            return wslot[:, sl], T("wslot", sl)

        def mm_fm(slot, slot_t, cc, rhs_buf, rhs_toks, col0, n, ps_ap, ps_tok):
            def f(e):
                for kc in range(KC):
                    ins = e.matmul(ps_ap[:, :n], lhsT=slot[:, kc, cc * 128:(cc + 1) * 128],
                                   rhs=rhs_buf[:, kc, col0:col0 + n], start=(kc == 0), stop=(kc == KC - 1))
                return ins
            fw.op("tensor", f, [slot_t] + rhs_toks, [ps_tok])

        for i in range(3):
            load_unit(i)
        ust["next_load"] = 3
        _ck[0] = 0
        for ps_ in (range(2) if not _KSTOP else [-1]):
            pass
        try:
          for ps_ in range(2):
              hasS = ps_ == 1
              tiles = [(0, 0, 512), (1, 512, 512)] + ([(2, 1024, 64)] if hasS else [])
              t128 = [(i, 128, i * 128) for i in range(8)] + ([(8, 64, 1024)] if hasS else [])

              def hT_toks(ti):
                  return [T("hT", 8)] if ti == 2 else [T("hT", 4 * ti + j) for j in range(4)]

              def ftoks(name, ti, cs):
                  return [T(name, ti, c) for c in cs]

              def xap(i, np_, c0=0, c1=D):
                  return xs_tok[0:np_, c0:c1] if i == 8 else x_tok[0:np_, i, c0:c1]

              if ps_ == 0:
                  for i in range(8):
                      r0 = ps_ * TPP + i * 128
                      fw.dma("sync", x_tok[:, i, :], xp[r0:r0 + 128, :], writes=[T("x", i)])
              if hasS:
                  fw.dma("sync", xs_tok[0:64, :], xs, writes=[T("x", 8)])

              for l in range(2):
                  vcol = lambda c, r: vec[:, l, c, r:r + 1]
                  with ExitStack() as bs:
                    if l == 0:
                      fw.es = bs
                      xn_ring = Ring(fw, "xn%d%d" % (ps_, l), [128, D], F32, 2)
                      nt128 = t128 if ps_ == 0 else [(8, 64, 1024)]
                      cs9 = slice(0, 9) if ps_ == 0 else slice(8, 9)
                      for (i, np_, c0) in nt128:
                          junk, jt = xn_ring.next()
                          fw.act(junk[0:np_, :], xap(i, np_), AF.Square, [T("x", i)], [jt, T("ss")],
                                 accum_out=ss[0:np_, i:i + 1])
                      fw.act(lnv_n[:, cs9], ss[:, cs9], AF.Ln, [T("ss")], [T("lnvn")], scale=1.0 / D, bias=EPS)
                      fw.act(rstd_n[:, cs9], lnv_n[:, cs9], AF.Exp, [T("lnvn")], [T("rstdn")], scale=-0.5)
                      for (i, np_, c0) in nt128:
                          xn, xt = xn_ring.next()
                          fw.ts(xn[0:np_, :], xap(i, np_), rstd_n[0:np_, i:i + 1], ALU.mult, [T("x", i), T("rstdn")], [xt])

                          def tr(e, xn=xn, np_=np_):
                              for kc in range(KC):
                                  ins = e.transpose(out=PT[:, kc // 4, (kc % 4) * 128:(kc % 4) * 128 + np_],
                                                    in_=xn[0:np_, kc * 128:(kc + 1) * 128],
                                                    identity=ident_f[0:np_, 0:np_])
                              return ins
                          fw.op("tensor", tr, [xt, T("con")], [T("PT", 0), T("PT", 1)])
                          for b in range(2):
                              fw.tt(hT[:, 4 * b:4 * b + 4, c0:c0 + np_],
                                    PT[:, b, :].rearrange("p (k t) -> p k t", k=4)[:, :, 0:np_],
                                    vec[:, l, 4 * b:4 * b + 4, V_NG:V_NG + 1].broadcast_to([128, 4, np_]),
                                    ALU.mult, [T("PT", b), T("vec")], [T("hT", i)])
                      fw.es = es
                  fw.barrier()

                  ck("normdone")
                  with ExitStack() as bs:
                      fw.es = bs
                      tg = "A%d%d" % (ps_, l)
                      nl = NL if hasS else NLMAX
                      u_bf = fw.sb("u_bf" + tg, [128, 8, 30 + TPP], BF16)
                      za = fw.sb("za" + tg, [128, 8, TC], BF16)
                      ca = fw.sb("ca" + tg, [128, 8, nl], F32)
                      sig_ring = Ring(fw, "sig" + tg, [128, 512], F32, 3)
                      dg_ring = Ring(fw, "dg" + tg, [128, 31, 128], BF16, 2)
                      sq_ring = Ring(fw, "sq" + tg, [128, nl], BF16, 3)
                      cb_ring = Ring(fw, "cb" + tg, [128, nl], BF16, 3)
                      st_mu = fw.sb("stmu" + tg, [128, nl], F32)
                      st_a = fw.sb("sta" + tg, [128, nl], F32)
                      st_rs = fw.sb("strs" + tg, [128, nl], F32)
                      st_nm = fw.sb("stnm" + tg, [128, nl], F32)
                      sl_ring = Ring(fw, "sl" + tg, [128, nl], F32, 2)
                      def build_dg(lb_, cs_=range(8)):
                          for c in cs_:
                              dgb, dgt = dg_ring.next()
                              fw.op("gpsimd", lambda e, dgb=dgb, c=c: e.affine_select(
                                  out=dgb[:], in_=vec[:, lb_, c, V_CAW:V_CAW + 31].unsqueeze(2).broadcast_to([128, 31, 128]),
                                  pattern=[[0, 31], [1, 128]], compare_op=ALU.is_equal, fill=0.0, base=0,
                                  channel_multiplier=-1), [T("vec")], [dgt])
                              fw.dma("sync", dgscr[lb_, c], dgb[:].rearrange("p k j -> p (k j)"), reads=[dgt],
                                     writes=[T("dgscr", lb_, c)])
                      if False:
                          for c in range(8):
                              dgb, dgt = dg_ring.next()
                              fw.op("gpsimd", lambda e, dgb=dgb, c=c: e.affine_select(
                                  out=dgb[:], in_=vec[:, l, c, V_CAW:V_CAW + 31].unsqueeze(2).broadcast_to([128, 31, 128]),
                                  pattern=[[0, 31], [1, 128]], compare_op=ALU.is_equal, fill=0.0, base=0,
                                  channel_multiplier=-1), [T("vec")], [dgt])
                              fw.dma("sync", dgscr[l, c], dgb[:].rearrange("p k j -> p (k j)"), reads=[dgt],
                                     writes=[T("dgscr", l, c)])
                      if hasS:
                          us_bf = fw.sb("us_bf" + tg, [128, 8, 16, 34], BF16)
                          us_new = fw.sb("us_new" + tg, [128, 8, 64], F32)
                          up_tail = fw.sb("up_tail" + tg, [128, 8, 30], F32)
                          asm_ring = Ring(fw, "asm" + tg, [128, 2, 16, 30], F32, 2)
                          fw.dma("gpsimd", us_bf[:].rearrange("p c j t -> p c (j t)"), scaP[:, l],
                                 writes=[T("us_bf")], max_dma_last_dim=2048)
                          fw.copy(u_bf[:, :, 0:30], utail[:, l], [T("utail", l)], [T("u_hist")])
                      else:
                          fw.op("vector", lambda e: e.memset(u_bf[:, :, 0:30], 0.0), [], [T("u_hist")])
                      for cp in range(4):
                          if ps_ == 0 and l == 0:
                              build_dg(0, range(2 * cp, 2 * cp + 2))
                          slot, st = next_unit("AGV")
                          for (ti, col0, n) in tiles:
                              for j in range(2):
                                  c = 2 * cp + j
                                  pg, pgt = pa_next()
                                  mm_fm(slot, st, j, hT, hT_toks(ti), col0, n, pg, pgt)
                                  sg, sgt = sig_ring.next()
                                  fw.act(sg[:, :n], pg[:, :n], AF.Sigmoid, [pgt], [sgt])
                                  pv, pvt = pa_next()
                                  mm_fm(slot, st, 2 + j, hT, hT_toks(ti), col0, n, pv, pvt)
                                  if ti < 2:
                                      fw.tt(u_bf[:, c, 30 + col0:30 + col0 + n], pv[:, :n], sg[:, :n], ALU.mult,
                                            [pvt, sgt], [T("u", ti, c)])
                                      if hasS and ti == 1:
                                          fw.tt(up_tail[:, c, :], pv[:, 482:512], sg[:, 482:512], ALU.mult,
                                                [pvt, sgt], [T("up_tail")])
                                  else:
                                      fw.tt(us_new[:, c, :], pv[:, :64], sg[:, :64], ALU.mult, [pvt, sgt],
                                            [T("us_new", c)])
                                      fw.copy(us_bf[:, c, :, 30:34],
                                              us_new[:, c, :].rearrange("p (j t) -> p j t", t=4),
                                              [T("us_new", c), T("us_bf")], [T("us", c)], eng="gpsimd")
                      if not hasS:
                          fw.copy(utail[:, l], u_bf[:, :, TPP:TPP + 30], [T("u", 1, c) for c in range(8)],
                                  [T("utail", l)])
                      ck("AGVdone")
                      ltiles = [((q * nl) // 512, q * nl, nl) for q in range(TPP // nl)] + ([(2, 1024, 64)] if hasS else [])
                      pend = None
                      for (ti, col0, n) in ltiles:
                          for c in range(8):
                              dgb, dgt = dg_ring.next()
                              fw.dma("sync", dgb[:].rearrange("p k j -> p (k j)"), dgscr[l, c], reads=[T("dgscr", l, c)],
                                     writes=[dgt])
                              pc, pct = pa_next()
                              if ti < 2:
                                  rtoks = [T("u", ti, c), T("u_hist")] + ([T("u", ti - 1, c)] if ti > 0 else [])

                                  def cv(e, dgb=dgb, c=c, col0=col0, n=n, pc=pc):
                                      for k in range(31):
                                          ins = e.matmul(pc[:, :n], lhsT=dgb[:, k, :], rhs=u_bf[:, c, col0 + k:col0 + k + n],
                                                         start=(k == 0), stop=(k == 30))
                                      return ins
                              else:
                                  rtoks = [T("us", c), T("us_bf")]

                                  def cv(e, dgb=dgb, c=c, pc=pc):
                                      for k in range(31):
                                          ins = e.matmul(pc[:, :64], lhsT=dgb[:, k, :], rhs=us_bf[:, c, :, k:k + 4],
                                                         start=(k == 0), stop=(k == 30))
                                      return ins
                              fw.op("tensor", cv, [dgt] + rtoks, [pct])
                              bcol = vcol(c, V_CAB)
                              fw.act(ca[:, c, :n], pc[:, :n], AF.Identity, [pct, T("vec")], [T("ca", c)], bias=bcol)
                              sq, sqt = sq_ring.next()
                              fw.act(sq[:, :n], pc[:, :n], AF.Square, [pct, T("vec")], [sqt], bias=bcol)
                              cb, cbt = cb_ring.next()
                              fw.act(cb[:, :n], pc[:, :n], AF.Identity, [pct, T("vec")], [cbt], bias=bcol)

                              def stm(e, cb=cb, sq=sq, c=c, n=n):
                                  e.matmul(PC[:, :n], lhsT=ones_b[:], rhs=cb[:, :n], start=(c == 0), stop=(c == 7))
                                  return e.matmul(PD[:, :n], lhsT=ones_b[:], rhs=sq[:, :n], start=(c == 0), stop=(c == 7))
                              if pend is not None:
                                  fw.op("tensor", pend[0], pend[1], [T("PC"), T("PD")])
                              pend = (stm, [cbt, sqt, T("onesb")])
                          fw.op("tensor", pend[0], pend[1], [T("PC"), T("PD")])
                          pend = None
                          fw.act(st_mu[:, :n], PC[:, :n], AF.Copy, [T("PC")], [T("stmu")], scale=1.0 / D)
                          fw.tt(st_a[:, :n], st_mu[:, :n], st_mu[:, :n], ALU.mult, [T("stmu")], [T("sta")])
                          fw.stt(st_a[:, :n], PD[:, :n], 1.0 / D, st_a[:, :n], ALU.mult, ALU.subtract,
                                 [T("PD"), T("sta")], [T("sta")])
                          fw.act(st_a[:, :n], st_a[:, :n], AF.Ln, [T("sta")], [T("sta")], bias=EPS)
                          fw.act(st_rs[:, :n], st_a[:, :n], AF.Exp, [T("sta")], [T("strs")], scale=-0.5)
                          fw.stt(st_nm[:, :n], st_mu[:, :n], -1.0, st_rs[:, :n], ALU.mult, ALU.mult,
                                 [T("stmu"), T("strs")], [T("stnm")])
                          for c in range(8):
                              fw.tt(ca[:, c, :n], ca[:, c, :n], st_rs[:, :n], ALU.mult, [T("ca", c), T("strs")], [T("ca", c)])
                              fw.tt(ca[:, c, :n], ca[:, c, :n], st_nm[:, :n], ALU.add, [T("ca", c), T("stnm")], [T("ca", c)])
                              fw.act(za[:, c, col0:col0 + n], ca[:, c, :n], AF.Silu, [T("ca", c), T("vec")], [T("za", ti, c)],
                                     scale=vcol(c, V_LNG), bias=vcol(c, V_LNB))
                      ck("convdone")
                      for h in range(2):
                          slot, st = next_unit("ZA")
                          for (ti, col0, n) in tiles:
                              for j in range(4):
                                  c = 4 * h + j
                                  pz, pzt = pa_next()
                                  mm_fm(slot, st, j, hT, hT_toks(ti), col0, n, pz, pzt)
                                  zs, zst = sig_ring.next()
                                  fw.act(zs[:, :n], pz[:, :n], AF.Silu, [pzt], [zst])
                                  fw.tt(za[:, c, col0:col0 + n], zs[:, :n], za[:, c, col0:col0 + n], ALU.mult,
                                        [zst, T("za", ti, c)], [T("za", ti, c)])
                      for cp in range(4):
                          if ps_ == 0 and l == 0:
                              build_dg(1, range(2 * cp, 2 * cp + 2))
                          slot, st = next_unit("AOG")
                          for (ti, col0, n) in tiles:
                              for j in range(2):
                                  c = 2 * cp + j
                                  pg, pgt = pa_next()
                                  mm_fm(slot, st, j, hT, hT_toks(ti), col0, n, pg, pgt)
                                  sg, sgt = sig_ring.next()
                                  fw.act(sg[:, :n], pg[:, :n], AF.Sigmoid, [pgt, T("vec")], [sgt], bias=vcol(c, V_GB))
                                  py, pyt = pa_next()
                                  mm_fm(slot, st, 2 + j, za, ftoks("za", ti, range(8)), col0, n, py, pyt)
                                  fw.tt(m[:, c, col0:col0 + n], py[:, :n], sg[:, :n], ALU.mult, [pyt, sgt],
                                        [T("m", ti, c)])
                      ck("AOGdone")
                      if hasS:
                          fw.dma("sync", napT[:, l], up_tail[:], reads=[T("up_tail")])
                          for cp in range(4):
                              ab, abt = asm_ring.next()
                              fw.dma("sync", ab[:].rearrange("p c j t -> p c (j t)"), scaS[:, l, 2 * cp:2 * cp + 2, :],
                                     writes=[abt])
                              for j in range(2):
                                  c = 2 * cp + j
                                  fw.copy(ab[:, j, :, 26:30], us_new[:, c, :].rearrange("p (j t) -> p j t", t=4),
                                          [T("us_new", c), abt], [abt], eng="gpsimd")
                              fw.dma("sync", nasT[:, l, 2 * cp:2 * cp + 2, :], ab[:].rearrange("p c j t -> p c (j t)"),
                                     reads=[abt])
                      fw.es = es
                  fw.barrier()

                  ck("Adone")
                  with ExitStack() as bs:
                      fw.es = bs
                      tg = "C%d%d" % (ps_, l)
                      cc = fw.sb("cc" + tg, [128, 8, TC], BF16)
                      chs_ring = Ring(fw, "chs" + tg, [128, 514], F32, 2)
                      cg_ring = Ring(fw, "cg" + tg, [128, 512], F32, 2)
                      acc_ring = Ring(fw, "acc" + tg, [128, 512], F32, 2)
                      gsc_ring = Ring(fw, "gsc" + tg, [128, 512], F32, 2)
                      tmp_ring = Ring(fw, "tmpc" + tg, [128, 512], BF16, 2)
                      if hasS:
                          chS = fw.sb("chS" + tg, [128, 8, 16, 6], F32)
                          ncs_sb = fw.sb("ncs_sb" + tg, [128, 8, 16, 2], F32)
                          fw.dma("sync", chS[:].rearrange("p c j t -> p c (j t)"), sccP[:, l], writes=[T("chS")])
                      for cp in range(4):
                          slot, st = next_unit("CGH")
                          for (ti, col0, n) in tiles:
                              for j in range(2):
                                  c = 2 * cp + j
                                  w0, w1, w2 = vcol(c, V_CCW), vcol(c, V_CCW + 1), vcol(c, V_CCW + 2)
                                  pg, pgt = pa_next()
                                  mm_fm(slot, st, j, hT, hT_toks(ti), col0, n, pg, pgt)
                                  cg, cgt = cg_ring.next()
                                  fw.act(cg[:, :n], pg[:, :n], AF.Copy, [pgt], [cgt])
                                  ph, pht = pa_next()
                                  mm_fm(slot, st, 2 + j, hT, hT_toks(ti), col0, n, ph, pht)
                                  ac, act_ = acc_ring.next()
                                  if ti < 2:
                                      chs, cht = chs_ring.next()
                                      fw.act(chs[:, 0:2], chtail[:, l, c, :], AF.Copy, [T("chtail", l)], [cht])
                                      fw.tt(chs[:, 2:2 + n], ph[:, :n], cg[:, :n], ALU.mult, [pht, cgt, cht], [cht])
                                      fw.act(chtail[:, l, c, :], chs[:, n:n + 2], AF.Copy, [cht], [T("chtail", l)])
                                      fw.ts(ac[:, :n], chs[:, 0:n], w0, ALU.mult, [cht, T("vec")], [act_])
                                      fw.stt(ac[:, :n], chs[:, 1:n + 1], w1, ac[:, :n], ALU.mult, ALU.add,
                                             [cht, act_, T("vec")], [act_])
                                      fw.stt(cc[:, c, col0:col0 + n], chs[:, 2:n + 2], w2, ac[:, :n], ALU.mult, ALU.add,
                                             [cht, act_, T("vec")], [T("cc", ti, c)])
                                  else:
                                      v3 = lambda a: a.rearrange("p (j t) -> p j t", t=4)
                                      fw.tt(chS[:, c, :, 2:6], v3(ph[:, :64]), v3(cg[:, :64]), ALU.mult,
                                            [pht, cgt, T("chS")], [T("chSc", c)])
                                      fw.ts(v3(ac[:, :64]), chS[:, c, :, 0:4], w0, ALU.mult, [T("chSc", c), T("vec")], [act_])
                                      fw.stt(v3(ac[:, :64]), chS[:, c, :, 1:5], w1, v3(ac[:, :64]), ALU.mult, ALU.add,
                                             [T("chSc", c), act_, T("vec")], [act_])
                                      fw.stt(v3(cc[:, c, 1024:1088]), chS[:, c, :, 2:6], w2, v3(ac[:, :64]), ALU.mult,
                                             ALU.add, [T("chSc", c), act_, T("vec")], [T("cc", ti, c)])
                      if hasS:
                          fw.dma("sync", ncpT[:, l], chtail[:, l], reads=[T("chtail", l)])
                          fw.copy(ncs_sb[:], chS[:, :, :, 4:6], [T("chSc", c) for c in range(8)], [T("ncs_sb")])
                          fw.dma("sync", ncsT[:, l], ncs_sb[:].rearrange("p c j t -> p c (j t)"), reads=[T("ncs_sb")])
                      for cp in range(4):
                          slot, st = next_unit("BGZ")
                          for (ti, col0, n) in tiles:
                              for j in range(2):
                                  c = 2 * cp + j
                                  pz, pzt = pa_next()
                                  mm_fm(slot, st, 2 + j, hT, hT_toks(ti), col0, n, pz, pzt)
                                  zs, zst = cg_ring.next()
                                  fw.act(zs[:, :n], pz[:, :n], AF.Silu, [pzt], [zst])
                                  pb, pbt = pa_next()
                                  mm_fm(slot, st, j, hT, hT_toks(ti), col0, n, pb, pbt)
                                  t1, t1t = acc_ring.next()
                                  fw.tt(t1[:, :n], pb[:, :n], zs[:, :n], ALU.mult, [pbt, zst], [t1t])
                                  fw.tt(cc[:, c, col0:col0 + n], t1[:, :n], cc[:, c, col0:col0 + n], ALU.mult,
                                        [t1t, T("cc", ti, c)], [T("cc", ti, c)])
                      for cp in range(4):
                          slot, st = next_unit("COG")
                          for (ti, col0, n) in tiles:
                              for j in range(2):
                                  c = 2 * cp + j
                                  pg, pgt = pa_next()
                                  mm_fm(slot, st, j, hT, hT_toks(ti), col0, n, pg, pgt)
                                  sg, sgt = gsc_ring.next()
                                  fw.act(sg[:, :n], pg[:, :n], AF.Sigmoid, [pgt, T("vec")], [sgt], bias=vcol(c, V_GB + 2))
                                  py, pyt = pa_next()
                                  mm_fm(slot, st, 2 + j, cc, ftoks("cc", ti, range(8)), col0, n, py, pyt)
                                  tm, tmt = tmp_ring.next()
                                  fw.tt(tm[:, :n], py[:, :n], sg[:, :n], ALU.mult, [pyt, sgt], [tmt])
                                  fw.tt(m[:, c, col0:col0 + n], m[:, c, col0:col0 + n], tm[:, :n], ALU.add,
                                        [tmt, T("m", ti, c)], [T("m", ti, c)])
                      fw.es = es
                  fw.barrier()

                  ck("Cdone")
                  with ExitStack() as bs:
                      fw.es = bs
                      tg = "B%d%d" % (ps_, l)
                      zb = fw.sb("zb" + tg, [128, 8, TC], BF16)
                      qT = fw.sb("qT" + tg, [128, 4, TC], BF16)
                      kT = fw.sb("kT" + tg, [128, 4, TC], BF16)
                      vtok = fw.sb("vtok" + tg, [128, 9, 512], BF16)
                      ebmid = fw.sb("ebmid" + tg, [128, 4, 8], F32)
                      eend = fw.sb("eend" + tg, [128, 4, 8], F32)
                      fr = {nm: Ring(fw, nm + tg, [128, 512], F32, (1 if (hasS and nm in ("fe", "fb", "fen")) else 2)) for nm in ("fe", "fl1", "fl2", "fb", "feb", "fen")}
                      Sp_f = fw.sb("Spf" + tg, [128, 4, 128], F32)
                      Sp_b = fw.sb("Spb" + tg, [128, 4, 128], BF16)
                      St2 = fw.sb("St2" + tg, [128, 4, 128], F32)
                      att_ring = Ring(fw, "att" + tg, [128, 512], BF16, 2)
                      kt_ring = Ring(fw, "ktk" + tg, [128, 512], BF16, 2)
                      osq_ring = Ring(fw, "osq" + tg, [128, 512], BF16, 1 if hasS else 2)
                      osb_ring = Ring(fw, "osb" + tg, [128, 512], F32, 1 if hasS else 2)
                      rs_ring = Ring(fw, "rsb" + tg, [128, 512], F32, 1 if hasS else 2)
                      gt_ring = Ring(fw, "gtb" + tg, [128, 512], BF16, 1 if hasS else 2)
                      if hasS:
                          eend_s = fw.sb("eends" + tg, [128, 4, 16], F32)
                          qs_f = fw.sb("qsf" + tg, [128, 4, 64], F32)
                          ktm = fw.sb("ktm" + tg, [64, 16, 128], BF16)
                          S0_ring = Ring(fw, "S0" + tg, [128, 4, 128], F32, 3)
                          So_ring = Ring(fw, "So" + tg, [128, 4, 128], F32, 2)
                      sl_i = [0]
                      for hg in range(2):
                          def fq_iter(slot, st, hp, ti, col0, n, j):
                              hl = hp * 2 + j
                              h = hg * 4 + hl
                              pf, pft = pa_next()
                              mm_fm(slot, st, j, hT, hT_toks(ti), col0, n, pf, pft)
                              fe, fet = fr["fe"].next()
                              fw.act(fe[:, :n], pf[:, :n], AF.Exp, [pft], [fet], scale=-1.0)
                              l1, l1t = fr["fl1"].next()
                              fw.act(l1[:, :n], fe[:, :n], AF.Ln, [fet, T("lbc")], [l1t], scale=lbc[:, l, h:h + 1], bias=1.0)
                              l2, l2t = fr["fl2"].next()
                              fw.act(l2[:, :n], fe[:, :n], AF.Ln, [fet], [l2t], bias=1.0)
                              fw.tt(l1[:, :n], l1[:, :n], l2[:, :n], ALU.subtract, [l1t, l2t], [l1t])
                              fb, fbt = fr["fb"].next()
                              rst = rstP if ti < 2 else rstS
                              fw.op("vector", lambda e, fb=fb, rst=rst, l1=l1, n=n: e.tensor_tensor_scan(
                                  out=fb[:, :n], data0=rst[:, :n], data1=l1[:, :n], initial=0.0,
                                  op0=ALU.mult, op1=ALU.add), [l1t, T("con")], [fbt])
                              if ti < 2:
                                  fw.act(ebmid[:, hl, ti * 4:ti * 4 + 4], fb[:, 63:512:128], AF.Exp, [fbt],
                                         [T("ebmid", hl)])
                                  fb3 = fb[:, :].rearrange("p (a t) -> p a t", t=128)
                                  l23 = l2[:, :].rearrange("p (a t) -> p a t", t=128)
                                  fw.tt(l23, fb3, fb3[:, :, 63:64].broadcast_to([128, 4, 128]), ALU.subtract,
                                        [fbt, l2t], [l2t])
                                  rsrc, rsrct = l2, l2t
                              else:
                                  rsrc, rsrct = fb, fbt
                              eb, ebt = fr["feb"].next()
                              fw.act(eb[:, :n], rsrc[:, :n], AF.Exp, [rsrct], [ebt])
                              en, ent = fr["fen"].next()
                              fw.act(en[:, :n], rsrc[:, :n], AF.Exp, [rsrct], [ent], scale=-1.0)
                              if ti < 2:
                                  fw.act(eend[:, hl, ti * 4:ti * 4 + 4], eb[:, 127:512:128], AF.Copy, [ebt],
                                         [T("eend", hl)])
                              else:
                                  fw.act(eend_s[:, hl, :], eb[:, 3:64:4], AF.Copy, [ebt], [T("eends", hl)])
                              fw.act(l2[:, :n], l1[:, :n], AF.Exp, [l1t, l2t], [l2t])
                              fw.ts(l2[:, :n], l2[:, :n], -1.0, ALU.mult, [l2t], [l2t], s2=1.0, op1=ALU.add)
                              fw.tt(kT[:, hl, col0:col0 + n], l2[:, :n], en[:, :n], ALU.mult, [l2t, ent],
                                    [T("kT", ti, hl)])
                              pq, pqt = pa_next()
                              mm_fm(slot, st, 2 + j, hT, hT_toks(ti), col0, n, pq, pqt)
                              fw.tt(qT[:, hl, col0:col0 + n], pq[:, :n], eb[:, :n], ALU.mult, [pqt, ebt],
                                    [T("qT", ti, hl)])
                              if ti == 2:
                                  fw.tt(qs_f[:, hl, :], pq[:, :64], eb[:, :64], ALU.mult, [pqt, ebt],
                                        [T("qsf", hl)])

                          def v_iter(slot, st, i, np_, c0, pv, pvt):
                              def vm(e, pv=pv, np_=np_, c0=c0, slot=slot):
                                  for kc in range(KC):
                                      ins = e.matmul(pv[0:np_, :], lhsT=hT[:, kc, c0:c0 + np_], rhs=slot[:, kc, :],
                                                     start=(kc == 0), stop=(kc == KC - 1))
                                  return ins
                              fw.op("tensor", vm, [st, T("hT", i)], [pvt])
                              if i % 2 == 0:
                                  fw.act(vtok[0:np_, i, :], pv[0:np_, :], AF.Copy, [pvt], [T("vtok", i)])
                              else:
                                  fw.copy(vtok[0:np_, i, :], pv[0:np_, :], [pvt], [T("vtok", i)])

                          slot, st = next_unit("FQ")
                          for (ti, col0, n) in tiles:
                              for j in range(2):
                                  fq_iter(slot, st, 0, ti, col0, n, j)
                          slot, st = next_unit("FQ")
                          slotv, stv = next_unit("V", ahead=1)
                          fq_list = [(ti, col0, n, j) for (ti, col0, n) in tiles for j in range(2)]
                          v_list = list(t128)
                          per = -(-len(v_list) // len(fq_list))
                          vk = 0
                          for (ti, col0, n, j) in fq_list:
                              fq_iter(slot, st, 1, ti, col0, n, j)
                              for _ in range(per):
                                  if v_list:
                                      (i, np_, c0) = v_list.pop(0)
                                      bank = (PC, T("PC")) if vk % 2 == 0 else (PD, T("PD"))
                                      vk += 1
                                      v_iter(slotv, stv, i, np_, c0, bank[0], bank[1])
                          while v_list:
                              (i, np_, c0) = v_list.pop(0)
                              bank = (PC, T("PC")) if vk % 2 == 0 else (PD, T("PD"))
                              vk += 1
                              v_iter(slotv, stv, i, np_, c0, bank[0], bank[1])
                          ck("FQdone")
                          ck("Vdone")
                          slot, st = next_unit("ZB")
                          for (ti, col0, n) in tiles:
                              for j in range(4):
                                  h = hg * 4 + j
                                  pz, pzt = pa_next()
                                  mm_fm(slot, st, j, hT, hT_toks(ti), col0, n, pz, pzt)
                                  fw.act(zb[:, h, col0:col0 + n], pz[:, :n], AF.Silu, [pzt], [T("zb", ti, h)])

                          def onorm(hl, ti, col0, n):
                              h = hg * 4 + hl
                              po, pot = PA[:, hl, :], T("PA", hl)
                              osq, osqt = osq_ring.next()
                              fw.act(osq[:, :n], po[:, :n], AF.Square, [pot], [osqt])
                              osb, osbt = osb_ring.next()
                              fw.act(osb[:, :n], po[:, :n], AF.Copy, [pot], [osbt])
                              fw.op("tensor", lambda e: e.matmul(PT[:, 1, :n], lhsT=ones_b[:], rhs=osq[:, :n],
                                                                 start=True, stop=True), [osqt, T("onesb")], [T("PT", 1)])
                              rs, rst_ = rs_ring.next()
                              fw.act(rs[:, :n], PT[:, 1, :n], AF.Ln, [T("PT", 1)], [rst_], scale=1.0 / 128, bias=EPS)
                              fw.act(rs[:, :n], rs[:, :n], AF.Exp, [rst_], [rst_], scale=-0.5)
                              fw.tt(osb[:, :n], osb[:, :n], rs[:, :n], ALU.mult, [osbt, rst_], [osbt])
                              fw.stt(zb[:, h, col0:col0 + n], osb[:, :n], vec[:, l, h, V_HG:V_HG + 1], zb[:, h, col0:col0 + n],
                                     ALU.mult, ALU.mult, [osbt, T("zb", ti, h), T("vec")], [T("zb", ti, h)])

                          ck("ZBdone")
                          H4 = range(4)
                          sft4 = [T("S_f", l, hg * 4 + q) for q in H4] + [T("S_f", l)]
                          bc = lambda a, j: a[:, :, j:j + 1].broadcast_to([128, 4, 128])
                          mp4 = maskpos.unsqueeze(1).broadcast_to([128, 4, 128])
                          mn4 = maskneg.unsqueeze(1).broadcast_to([128, 4, 128])
                          m14 = mask01.unsqueeze(1).broadcast_to([128, 4, 128])
                          ebt4 = [T("ebmid", q) for q in H4]
                          eet4 = [T("eend", q) for q in H4]
                          fw.tt(Sp_f[:], S_f[:, l, hg * 4:hg * 4 + 4, :], bc(ebmid, 0), ALU.mult, sft4 + ebt4, [T("Spf")])
                          fw.copy(Sp_b[:], Sp_f[:], [T("Spf")], [T("Spb")])
                          for i in range(8):
                              c0 = i * 128
                              ti = i // 4
                              kq = [T("kT", ti, q) for q in H4] + [T("qT", ti, q) for q in H4]

                              def att4(e, c0=c0):
                                  for q in H4:
                                      e.matmul(PC[0:64, q * 128:q * 128 + 128], lhsT=kT[:, q, c0:c0 + 64],
                                               rhs=qT[:, q, c0:c0 + 128], start=True, stop=True)
                                      ins = e.matmul(PC[64:128, q * 128 + 64:q * 128 + 128], lhsT=kT[:, q, c0 + 64:c0 + 128],
                                                     rhs=qT[:, q, c0 + 64:c0 + 128], start=True, stop=True)
                                  return ins
                              fw.op("tensor", att4, kq, [T("PC")])
                              at, att_ = att_ring.next()
                              fw.tt(at[:].rearrange("p (a t) -> p a t", t=128), PC[:].rearrange("p (a t) -> p a t", t=128),
                                    m14, ALU.mult, [T("PC"), T("con")], [att_])

                              def tr4(e, c0=c0):
                                  for q in H4:
                                      ins = e.transpose(out=PTb[:, q * 128:q * 128 + 128], in_=kT[:, q, c0:c0 + 128],
                                                        identity=ident_b[:])
                                  return ins
                              fw.op("tensor", tr4, kq + [T("identb")], [T("PT", 0)])
                              kt, ktt = kt_ring.next()
                              fw.act(kt[:], PTb[:, 0:512], AF.Copy, [T("PT", 0)], [ktt])
                              def kv4(e, i=i, kt=kt):
                                  for q in H4:
                                      ins = e.matmul(PD[:, q * 128:q * 128 + 128], lhsT=kt[:, q * 128:q * 128 + 128],
                                                     rhs=vtok[:, i, q * 128:q * 128 + 128], start=True, stop=True)
                                  return ins
                              fw.op("tensor", kv4, [ktt, T("vtok", i)], [T("PD")])
                              oc = slice((i % 4) * 128, (i % 4) * 128 + 128)

                              def om4(e, i=i, at=at, c0=c0, oc=oc):
                                  for q in H4:
                                      e.matmul(PA[:, q, oc], lhsT=vtok[:, i, q * 128:q * 128 + 128],
                                               rhs=at[:, q * 128:q * 128 + 128], start=True, stop=False)
                                      ins = e.matmul(PA[:, q, oc], lhsT=Sp_b[:, q, :], rhs=qT[:, q, c0:c0 + 128],
                                                     start=False, stop=True)
                                  return ins
                              fw.op("tensor", om4, [T("vtok", i), att_, T("Spb")] + kq, [T("PA", q) for q in H4])

                              PD3 = PD[:].rearrange("p (a t) -> p a t", t=128)
                              fw.tt(St2[:], Sp_f[:], PD3, ALU.add, [T("Spf"), T("PD")], [T("St2")])
                              if i < 7:
                                  fw.tt(St2[:], St2[:], bc(eend, i), ALU.mult, [T("St2")] + eet4, [T("St2")])
                                  fw.tt(Sp_b[:], St2[:], bc(ebmid, i + 1), ALU.mult, [T("St2")] + ebt4, [T("Spb")])
                                  fw.tt(Sp_f[:], St2[:], bc(ebmid, i + 1), ALU.mult, [T("St2")] + ebt4, [T("Spf")])
                              else:
                                  fw.tt(S_f[:, l, hg * 4:hg * 4 + 4, :], St2[:], bc(eend, i), ALU.mult, [T("St2")] + eet4,
                                        [T("S_f", l, hg * 4 + q) for q in H4])
                              if i % 4 == 3:
                                  for hl in range(4):
                                      onorm(hl, ti, ti * 512, 512)
                          ck("recdone")
                          if hasS and 'S' not in _KSKIP:
                              items = [(hl, jb) for hl in range(4) for jb in range(4)]
                              loaded = {}

                              def s0_load(idx):
                                  hl_, jb_ = items[idx]
                                  s0, s0t = S0_ring.next()
                                  fw.dma("sync", s0[:], shg[l, jb_ * 4:jb_ * 4 + 4, hg * 4 + hl_].rearrange("j k v -> k j v"),
                                         writes=[s0t])
                                  loaded[idx] = (s0, s0t)
                              PF = 2
                              for idx in range(min(PF, len(items))):
                                  s0_load(idx)
                              for hl in range(4):
                                  h = hg * 4 + hl
                                  fw.op("tensor", lambda e, hl=hl: e.matmul(
                                      PC[0:64, 0:64], lhsT=kT[:, hl, 1024:1088], rhs=qT[:, hl, 1024:1088], start=True, stop=True),
                                      [T("kT", 2, hl), T("qT", 2, hl)], [T("PC")])
                                  at, att_ = att_ring.next()
                                  fw.tt(at[0:64, 0:64], PC[0:64, 0:64], maskS, ALU.mult, [T("PC"), T("con")], [att_])
                                  fw.op("tensor", lambda e, hl=hl: e.transpose(
                                      out=PTb[0:64, 0:128], in_=kT[:, hl, 1024:1088], identity=ident_b[:]),
                                      [T("kT", 2, hl), T("identb")], [T("PT", 0)])
                                  kt, ktt = kt_ring.next()
                                  fw.act(kt[0:64, 0:128], PTb[0:64, 0:128], AF.Copy, [T("PT", 0)], [ktt])
                                  fw.tt(ktm[:, :, :], kt[0:64, 0:128].unsqueeze(1).broadcast_to([64, 16, 128]),
                                        seqm.unsqueeze(2).broadcast_to([64, 16, 128]), ALU.mult, [ktt, T("con")],
                                        [T("ktm")])
                                  fw.op("tensor", lambda e, hl=hl, at=at: e.matmul(
                                      PA[:, hl, 0:64], lhsT=vtok[0:64, 8, hl * 128:hl * 128 + 128], rhs=at[0:64, 0:64],
                                      start=True, stop=False), [T("vtok", 8), att_], [T("PA", hl)])
                                  for jb in range(4):
                                      idx = hl * 4 + jb
                                      if idx + PF < len(items):
                                          s0_load(idx + PF)
                                      s0, s0t = loaded.pop(idx)
                                      so, sot = So_ring.next()
                                      def sm(e, hl=hl, jb=jb, s0=s0):
                                          for jj in range(4):
                                              j = jb * 4 + jj
                                              ins = e.matmul(PA[:, hl, 4 * j:4 * j + 4], lhsT=s0[:, jj, :],
                                                             rhs=qs_f[:, hl, 4 * j:4 * j + 4], start=False, stop=(j == 15))
                                          return ins
                                      fw.op("tensor", sm, [s0t, T("qsf", hl)], [T("PA", hl)])

                                      def kvs(e, hl=hl, jb=jb):
                                          for jj in range(4):
                                              j = jb * 4 + jj
                                              ins = e.matmul(PD[:, jj * 128:jj * 128 + 128], lhsT=ktm[:, j, :],
                                                             rhs=vtok[0:64, 8, hl * 128:hl * 128 + 128], start=True, stop=True)
                                          return ins
                                      fw.op("tensor", kvs, [T("ktm"), T("vtok", 8)], [T("PD")])
                                      fw.tt(St2[:], s0[:], PD[:].rearrange("p (a t) -> p a t", t=128), ALU.add,
                                            [s0t, T("PD")], [T("St2")])
                                      fw.tt(so[:], St2[:], eend_s[:, hl, jb * 4:jb * 4 + 4].unsqueeze(2).broadcast_to([128, 4, 128]),
                                            ALU.mult, [T("St2"), T("eends", hl)], [sot])
                                      fw.dma("sync", nhs[l, jb * 4:jb * 4 + 4, h].rearrange("j k v -> k j v"), so[:],
                                             reads=[sot])
                                  onorm(hl, 2, 1024, 64)
                      if hasS:
                          fw.dma("sync", nhp[l].rearrange("h k v -> k h v"), S_f[:, l], reads=[T("S_f", l, h) for h in range(8)])
                      ck("Sdone")
                      for cp in range(4):
                          slot, st = next_unit("BOG")
                          for (ti, col0, n) in tiles:
                              for j in range(2):
                                  c = 2 * cp + j
                                  pg, pgt = pa_next()
                                  mm_fm(slot, st, j, hT, hT_toks(ti), col0, n, pg, pgt)
                                  sg, sgt = rs_ring.next()
                                  fw.act(sg[:, :n], pg[:, :n], AF.Sigmoid, [pgt, T("vec")], [sgt], bias=vcol(c, V_GB + 1))
                                  py, pyt = pa_next()
                                  mm_fm(slot, st, 2 + j, zb, ftoks("zb", ti, range(8)), col0, n, py, pyt)
                                  tm, tmt = gt_ring.next()
                                  fw.tt(tm[:, :n], py[:, :n], sg[:, :n], ALU.mult, [pyt, sgt], [tmt])
                                  fw.tt(m[:, c, col0:col0 + n], m[:, c, col0:col0 + n], tm[:, :n], ALU.add,
                                        [tmt, T("m", ti, c)], [T("m", ti, c)])
                      fw.es = es
                  fw.barrier()

                  ck("Bdone")
                  with ExitStack() as bs:
                      fw.es = bs
                      xn_ring = Ring(fw, "xo%d%d" % (ps_, l), [128, D], F32, 3)
                      yo_ring = Ring(fw, "yo%d%d" % (ps_, l), [128, D], F32, 2) if l == 1 else None
                      pend_tr = []

                      def flush_tr(keep, gl=1):
                          while len(pend_tr) > keep:
                              (xn, xt, i, np_, c0) = pend_tr.pop(0)

                              def tr(e, xn=xn, np_=np_):
                                  for kc in range(KC):
                                      ins = e.transpose(out=PT[:, kc // 4, (kc % 4) * 128:(kc % 4) * 128 + np_],
                                                        in_=xn[0:np_, kc * 128:(kc + 1) * 128],
                                                        identity=ident_f[0:np_, 0:np_])
                                  return ins
                              fw.op("tensor", tr, [xt, T("con")], [T("PT", 0), T("PT", 1)])
                              for b_ in range(2):
                                  fw.tt(hT[:, 4 * b_:4 * b_ + 4, c0:c0 + np_],
                                        PT[:, b_, :].rearrange("p (k t) -> p k t", k=4)[:, :, 0:np_],
                                        vec[:, gl, 4 * b_:4 * b_ + 4, V_NG:V_NG + 1].broadcast_to([128, 4, np_]),
                                        ALU.mult, [T("PT", b_), T("vec"), xt], [T("hT", i)])
                      pend_pre = []

                      def flush_pre(keep):
                          while len(pend_pre) > keep:
                              (i, c0) = pend_pre.pop(0)
                              xn2, xt2 = xn_ring.next()
                              fw.act(xn2[:, :], x_tok[:, i, :], AF.Square, [T("x", i)], [xt2, T("ss", i)],
                                     accum_out=ss[:, i:i + 1])
                              fw.act(lnv_n[:, i:i + 1], ss[:, i:i + 1], AF.Ln, [T("ss", i)], [T("lnvn", i)],
                                     scale=1.0 / D, bias=EPS)
                              fw.act(rstd_n[:, i:i + 1], lnv_n[:, i:i + 1], AF.Exp, [T("lnvn", i)],
                                     [T("rstdn", i)], scale=-0.5)
                              fw.ts(xn2[:, :], x_tok[:, i, :], rstd_n[:, i:i + 1], ALU.mult,
                                    [T("x", i), T("rstdn", i)], [xt2])
                              pend_tr.append((xn2, xt2, i, 128, c0))
                              flush_tr(2, 0)
                      for hf in range(2):
                          slot, st = next_unit("WO")
                          for (i, np_, c0) in t128:
                              po, pot = pa_next()
                              ti = 2 if i == 8 else i // 4

                              def wm(e, po=po, np_=np_, c0=c0, slot=slot):
                                  for kc in range(KC):
                                      ins = e.matmul(po[0:np_, :], lhsT=m[:, kc, c0:c0 + np_], rhs=slot[:, kc, :],
                                                     start=(kc == 0), stop=(kc == KC - 1))
                                  return ins
                              fw.op("tensor", wm, [st] + [T("m", ti, c) for c in range(8)], [pot])
                              xa = xap(i, np_, hf * 512, hf * 512 + 512)
                              fw.tt(xa, po[0:np_, :], xa, ALU.add, [pot, T("x", i)], [T("x", i)])
                              if hf == 1:
                                  xn, xt = (xn_ring if l == 0 else yo_ring).next()
                                  fw.act(xn[0:np_, :], xap(i, np_), AF.Square, [T("x", i)], [xt, T("ss", i)],
                                         accum_out=ss[0:np_, i:i + 1])
                                  fw.act(lnv_n[0:np_, i:i + 1], ss[0:np_, i:i + 1], AF.Ln, [T("ss", i)], [T("lnvn", i)],
                                         scale=1.0 / D, bias=EPS)
                                  fw.act(rstd_n[0:np_, i:i + 1], lnv_n[0:np_, i:i + 1], AF.Exp, [T("lnvn", i)],
                                         [T("rstdn", i)], scale=-0.5)
                                  if l == 0:
                                      fw.ts(xn[0:np_, :], xap(i, np_), rstd_n[0:np_, i:i + 1], ALU.mult,
                                            [T("x", i), T("rstdn", i)], [xt])
                                      pend_tr.append((xn, xt, i, np_, c0))
                                      flush_tr(2)
                                  else:
                                      fw.stt(xn[0:np_, :], xap(i, np_), rstd_n[0:np_, i:i + 1], fng[0:np_, :], ALU.mult,
                                             ALU.mult, [T("x", i), T("rstdn", i), T("fng")], [xt])
                                      if i == 8:
                                          fw.dma("sync", ys, xn[0:64, :], reads=[xt])
                                      else:
                                          r0 = ps_ * TPP + i * 128
                                          fw.dma("sync", yp[r0:r0 + 128, :], xn[:], reads=[xt])
                                      if ps_ == 0:
                                          r1 = (ps_ + 1) * TPP + i * 128
                                          fw.dma("sync", x_tok[:, i, :], xp[r1:r1 + 128, :], writes=[T("x", i)])
                                          pend_pre.append((i, c0))
                                          flush_pre(3)
                      flush_pre(0)
                      flush_tr(0, 1 if l == 0 else 0)
                      fw.es = es
                  fw.barrier()
              ck("Odone")
        except StopBuild:
            fw.es = es
        assert _KSTOP or ust["next_use"] == len(specs)
        fw.finish_all()
        print("[build] ops=%d waits=%d" % (fw.nops, fw.nwaits))
    return nc


def _consts():
    c = np.zeros((128, NCON), np.float32)
    c[:, K_ID:K_ID + 128] = np.eye(128, dtype=np.float32)
    s = np.arange(128)[:, None]
    t = np.arange(128)[None, :]
    keep = (s <= t)
    c[:, K_MP:K_MP + 128] = np.where(keep, BIG, 0.0)
    c[:, K_MN:K_MN + 128] = np.where(keep, -BIG, 0.0)
    c[:, K_M1:K_M1 + 128] = np.where(keep, 1.0, 0.0)
    s6 = np.arange(64)[:, None]
    t6 = np.arange(64)[None, :]
    c[0:64, K_MS:K_MS + 64] = ((s6 <= t6) & (s6 // 4 == t6 // 4)).astype(np.float32)
    c[:, K_RP:K_RP + 512] = (np.arange(512) % 128 != 0).astype(np.float32)[None, :]
    c[:, K_RS:K_RS + 64] = (np.arange(64) % 4 != 0).astype(np.float32)[None, :]
    c[0:64, K_SM:K_SM + 16] = (np.arange(64)[:, None] // 4 == np.arange(16)[None, :]).astype(np.float32)
    return c


_NC_CACHE = {}


def kernel(x_prompt, x_sample, state_conv_a, state_hgrn, state_conv_c, norm_g, w_in, gate_b, conv_a_w,
           conv_a_b, ln_g, ln_b, w_a_out, lb_raw, hg_norm_g, w_b_out, conv_c_w, w_c_out, w_o, final_norm_g):
    f = lambda a: np.ascontiguousarray(np.asarray(a, dtype=np.float32))
    x_prompt, x_sample, state_conv_a, state_hgrn, state_conv_c = map(f, (x_prompt, x_sample, state_conv_a,
                                                                        state_hgrn, state_conv_c))
    vl = []
    for l in range(2):
        rows = [f(lb_raw)[0], f(lb_raw)[1], f(norm_g)[l]] + list(f(gate_b)[l].reshape(3, D)) + \
               [f(conv_a_b)[l], f(ln_g)[l], f(ln_b)[l], f(hg_norm_g)[l]] + list(f(conv_c_w)[l]) + list(f(conv_a_w)[l])
        r = np.stack(rows)
        assert r.shape[0] == NV
        vl.append(r.reshape(NV, 8, 128).transpose(2, 1, 0))
    vecs = f(np.stack(vl, axis=1).reshape(128, 2 * 8 * NV))
    fngb = f(np.broadcast_to(f(final_norm_g)[None, :], (128, D)))
    cons = _consts()
    wts = {"w_in": f(w_in), "w_a_out": f(w_a_out), "w_b_out": f(w_b_out), "w_c_out": f(w_c_out), "w_o": f(w_o)}
    in_maps = []
    for i in range(8):
        sl = slice(16 * i, 16 * i + 16)
        sca = state_conv_a[:, sl].reshape(2, 16, 30, 8, 128).transpose(4, 0, 3, 1, 2)
        scaP = np.zeros((128, 2, 8, 16, 34), np.float32)
        scaP[..., 0:30] = sca
        scaS = np.zeros((128, 2, 8, 16, 30), np.float32)
        scaS[..., 0:26] = sca[..., 4:30]
        scc = state_conv_c[:, sl].reshape(2, 16, 2, 8, 128).transpose(4, 0, 3, 1, 2)
        sccP = np.zeros((128, 2, 8, 16, 6), np.float32)
        sccP[..., 0:2] = scc
        d = {"xp": f(x_prompt[i]), "xs": f(x_sample[sl].reshape(TS, D)),
             "scaP": f(scaP.reshape(128, 2, 8, 16 * 34)), "scaS": f(scaS.reshape(128, 2, 8, 16 * 30)),
             "shg": f(state_hgrn[:, sl]), "sccP": f(sccP.reshape(128, 2, 8, 96)),
             "vecs": vecs, "fngb": fngb, "cons": cons}
        d.update(wts)
        in_maps.append(d)
    if "nc" not in _NC_CACHE:
        _NC_CACHE["nc"] = build_program()
    res = run_bass_kernel_spmd(_NC_CACHE["nc"], in_maps, core_ids=list(range(8)))
    R = res.results
    y_prompt = np.stack([R[i]["yp"] for i in range(8)]).astype(np.float32)
    y_sample = np.concatenate([R[i]["ys"].reshape(16, 4, D) for i in range(8)], axis=0).astype(np.float32)
    nap = np.stack([R[i]["napT"].transpose(1, 3, 2, 0).reshape(2, 30, D) for i in range(8)], axis=1)
    nhp_ = np.stack([R[i]["nhp"] for i in range(8)], axis=1)
    ncp = np.stack([R[i]["ncpT"].transpose(1, 3, 2, 0).reshape(2, 2, D) for i in range(8)], axis=1)
    nas = np.concatenate([R[i]["nasT"].reshape(128, 2, 8, 16, 30).transpose(1, 3, 4, 2, 0).reshape(2, 16, 30, D)
                          for i in range(8)], axis=1)
    nhs_ = np.concatenate([R[i]["nhs"] for i in range(8)], axis=1)
    ncs = np.concatenate([R[i]["ncsT"].reshape(128, 2, 8, 16, 2).transpose(1, 3, 4, 2, 0).reshape(2, 16, 2, D)
                          for i in range(8)], axis=1)
    c = lambda a: np.ascontiguousarray(a, dtype=np.float32)
    return (c(y_prompt), c(y_sample), c(nap), c(nhp_), c(ncp), c(nas), c(nhs_), c(ncs))
```

```python
import numpy as np
import concourse.bass as bass
import concourse.mybir as mybir
from concourse.bass_utils import run_bass_kernel_spmd
from contextlib import ExitStack

F32 = mybir.dt.float32
BF16 = mybir.dt.bfloat16
AF = mybir.ActivationFunctionType
ALU = mybir.AluOpType

D = 1024
KC = 8
TPP = 1024
TS = 64
TC = TPP + TS
EPS = 1e-6
BIG = 3.0e38
NL = 256
NLMAX = 512
(C_AVAL, C_AGATE, C_ZA, C_Q, C_F, C_V, C_ZB, C_BG, C_CG, C_HC, C_ZC, C_G) = [i * 1024 for i in range(12)]
V_LB0, V_LB1, V_NG, V_GB, V_CAB, V_LNG, V_LNB, V_HG, V_CCW, V_CAW = 0, 1, 2, 3, 6, 7, 8, 9, 10, 13
NV = 44
K_ID, K_MP, K_MN, K_MS, K_RP, K_RS, K_SM, K_M1, NCON = 0, 128, 256, 384, 448, 960, 1024, 1040, 1168


class Tok:
    __slots__ = ("name", "w", "r")

    def __init__(self, name):
        self.name = name
        self.w = None
        self.r = []


class FW:
    ENGS = ("tensor", "vector", "scalar", "gpsimd", "sync")
    NDMA = 10

    def __init__(self, nc, es):
        self.nc = nc
        self.es = es
        self.eng = {n: getattr(nc, n) for n in self.ENGS}
        self.sem = {}
        self.cnt = {}
        for n in self.ENGS:
            self.sem[n] = es.enter_context(nc.semaphore("s_" + n))
            self.cnt[n] = 0
        self.dnext = {}
        for q in ("sync", "gpsimd"):
            for i in range(self.NDMA):
                k = "d_%s_%d" % (q, i)
                self.sem[k] = es.enter_context(nc.semaphore(k))
                self.cnt[k] = 0
            self.dnext[q] = 0
        self.waited = {}
        self.tk = {}
        self.nwaits = 0
        self.nops = 0
        self.nuniq = 0
        self.bar_tile = es.enter_context(nc.sbuf_tensor("bar_tile", [128, 8], F32))
        self.bar_ev = None

    def T(self, *key):
        t = self.tk.get(key)
        if t is None:
            t = Tok(str(key))
            self.tk[key] = t
        return t

    def sb(self, name, shape, dt):
        return self.es.enter_context(self.nc.sbuf_tensor(name, list(shape), dt))

    def ps(self, name, shape, dt=F32):
        return self.es.enter_context(self.nc.psum_tensor(name, list(shape), dt))

    def _need(self, en, ev, out):
        if ev is None:
            return
        k, v = ev
        if self.waited.get((en, k), 0) >= v:
            return
        if out.get(k, 0) < v:
            out[k] = v

    def _deps(self, en, reads, writes):
        need = {}
        for t in reads:
            self._need(en, t.w, need)
            if t.name.startswith("('P"):
                for ev in t.r:
                    if ev[0] != en:
                        self._need(en, ev, need)
        for t in writes:
            self._need(en, t.w, need)
            for ev in t.r:
                self._need(en, ev, need)
        return need

    def _emit_waits(self, en, need):
        e = self.eng[en]
        for k, v in need.items():
            e.wait_ge(self.sem[k], v)
            self.waited[(en, k)] = v
            self.nwaits += 1

    def _record(self, ev, reads, writes):
        for t in reads:
            if len(t.r) > 24:
                best = {}
                for k, v in t.r:
                    if best.get(k, 0) < v:
                        best[k] = v
                t.r = list(best.items())
            t.r.append(ev)
        for t in writes:
            t.w = ev
            t.r = []

    def op(self, en, fn, reads=(), writes=()):
        need = self._deps(en, reads, writes)
        if en == "tensor":
            need.pop("tensor", None)
        self._emit_waits(en, need)
        ins = fn(self.eng[en])
        self.cnt[en] += 1
        ins.then_inc(self.sem[en], 1)
        ev = (en, self.cnt[en])
        self._record(ev, reads, writes)
        self.nops += 1
        return ev

    def dma(self, q, out, in_, reads=(), writes=(), **kw):
        i = self.dnext[q]
        self.dnext[q] = (i + 1) % self.NDMA
        k = "d_%s_%d" % (q, i)
        need = self._deps(q, reads, writes)
        if self.cnt[k] > 0:
            self._need(q, (k, self.cnt[k]), need)
        self._emit_waits(q, need)
        ins = self.eng[q].dma_start(out=out, in_=in_, **kw)
        self.cnt[k] += 16
        ins.then_inc(self.sem[k], 16)
        ev = (k, self.cnt[k])
        self._record(ev, reads, writes)
        self.nops += 1
        return ev

    def barrier(self):
        need = {}
        for o in ("tensor", "scalar", "gpsimd"):
            if self.cnt[o] > 0:
                self._need("vector", (o, self.cnt[o]), need)
        for k, v in self.cnt.items():
            if k.startswith("d_sync_") and v > 0:
                self._need("vector", (k, v), need)
        if self.bar_ev is not None:
            self._need("vector", self.bar_ev, need)
        self._emit_waits("vector", need)
        ins = self.eng["vector"].memset(self.bar_tile[:], 0.0)
        self.cnt["vector"] += 1
        ins.then_inc(self.sem["vector"], 1)
        ev = ("vector", self.cnt["vector"])
        self.bar_ev = ev
        for en in ("tensor", "scalar", "gpsimd", "sync"):
            nd = {}
            self._need(en, ev, nd)
            self._emit_waits(en, nd)

    def finish_all(self):
        need = {}
        for k, v in self.cnt.items():
            if v > 0 and k != "sync":
                self._need("sync", (k, v), need)
        self._emit_waits("sync", need)

    def act(self, out, in_, func, reads, writes, scale=1.0, bias=0.0, accum_out=None):
        kw = {}
        if accum_out is not None:
            kw["accum_out"] = accum_out
        return self.op("scalar", lambda e: e.activation(out=out, in_=in_, func=func, scale=scale, bias=bias, **kw),
                       reads, writes)

    def tt(self, out, in0, in1, op, reads, writes, eng="vector"):
        return self.op(eng, lambda e: e.tensor_tensor(out=out, in0=in0, in1=in1, op=op), reads, writes)

    def ts(self, out, in0, s1, op0, reads, writes, s2=None, op1=None, eng="vector"):
        if op1 is None:
            return self.op(eng, lambda e: e.tensor_scalar(out=out, in0=in0, scalar1=s1, scalar2=None, op0=op0),
                           reads, writes)
        return self.op(eng, lambda e: e.tensor_scalar(out=out, in0=in0, scalar1=s1, scalar2=s2, op0=op0, op1=op1),
                       reads, writes)

    def stt(self, out, in0, scalar, in1, op0, op1, reads, writes):
        return self.op("vector", lambda e: e.scalar_tensor_tensor(out=out, in0=in0, scalar=scalar, in1=in1,
                                                                  op0=op0, op1=op1), reads, writes)

    def copy(self, out, in_, reads, writes, eng="vector"):
        return self.op(eng, lambda e: e.tensor_copy(out=out, in_=in_), reads, writes)


class StopBuild(Exception):
    pass


import os as _os
_KSTOP = int(_os.environ.get("KSTOP", "0"))
_KSKIP = _os.environ.get("KSKIP", "")
_ck = [0]


def ck(name):
    _ck[0] += 1
    if _KSTOP and _ck[0] >= _KSTOP:
        print("[build] stopping at checkpoint %d (%s)" % (_ck[0], name))
        raise StopBuild()


class Ring:
    def __init__(self, fw, name, shape, dt, n):
        self.bufs = [fw.sb("%s%d" % (name, i), shape, dt) for i in range(n)]
        self.toks = [fw.T(name, i) for i in range(n)]
        self.i = 0

    def next(self):
        b, t = self.bufs[self.i], self.toks[self.i]
        self.i = (self.i + 1) % len(self.bufs)
        return b, t


def unit_specs(l, W):
    w_in, w_a, w_b, w_c, w_o = W["w_in"], W["w_a_out"], W["w_b_out"], W["w_c_out"], W["w_o"]
    s = []
    wi = lambda c0, n: w_in[l, :, c0:c0 + n]
    for cp in range(4):
        s.append(("AGV", [wi(C_AGATE + cp * 256, 256), wi(C_AVAL + cp * 256, 256)]))
    for h in range(2):
        s.append(("ZA", [wi(C_ZA + h * 512, 512)]))
    for cp in range(4):
        s.append(("AOG", [wi(C_G + cp * 256, 256), w_a[l, :, cp * 256:cp * 256 + 256]]))
    for cp in range(4):
        s.append(("CGH", [wi(C_CG + cp * 256, 256), wi(C_HC + cp * 256, 256)]))
    for cp in range(4):
        s.append(("BGZ", [wi(C_BG + cp * 256, 256), wi(C_ZC + cp * 256, 256)]))
    for cp in range(4):
        s.append(("COG", [wi(C_G + 2048 + cp * 256, 256), w_c[l, :, cp * 256:cp * 256 + 256]]))
    for hg in range(2):
        for hp in range(2):
            h0 = hg * 4 + hp * 2
            s.append(("FQ", [wi(C_F + h0 * 128, 256), wi(C_Q + h0 * 128, 256)]))
        s.append(("V", [wi(C_V + hg * 512, 512)]))
        s.append(("ZB", [wi(C_ZB + hg * 512, 512)]))
    for cp in range(4):
        s.append(("BOG", [wi(C_G + 1024 + cp * 256, 256), w_b[l, :, cp * 256:cp * 256 + 256]]))
    for h in range(2):
        s.append(("WO", [w_o[l, :, h * 512:h * 512 + 512]]))
    return s


def build_program():
    nc = bass.Bass("TRN2", target_bir_lowering=False)

    def din(name, shape):
        return nc.dram_tensor(name, list(shape), F32, kind="ExternalInput").ap()

    def dout(name, shape):
        return nc.dram_tensor(name, list(shape), F32, kind="ExternalOutput").ap()

    xp = din("xp", [2048, D])
    xs = din("xs", [TS, D])
    scaP = din("scaP", [128, 2, 8, 16 * 34])
    scaS = din("scaS", [128, 2, 8, 16 * 30])
    shg = din("shg", [2, 16, 8, 128, 128])
    sccP = din("sccP", [128, 2, 8, 16 * 6])
    vecs = din("vecs", [128, 2 * 8 * NV])
    fngb = din("fngb", [128, D])
    cons = din("cons", [128, NCON])
    W = {"w_in": din("w_in", [2, D, 14336])}
    for n in ("w_a_out", "w_b_out", "w_c_out", "w_o"):
        W[n] = din(n, [2, D, D])
    yp = dout("yp", [2048, D])
    ys = dout("ys", [TS, D])
    napT = dout("napT", [128, 2, 8, 30])
    nhp = dout("nhp", [2, 8, 128, 128])
    ncpT = dout("ncpT", [128, 2, 8, 2])
    nasT = dout("nasT", [128, 2, 8, 16 * 30])
    nhs = dout("nhs", [2, 16, 8, 128, 128])
    ncsT = dout("ncsT", [128, 2, 8, 16 * 2])
    dgscr = nc.dram_tensor("dgscr", [2, 8, 128, 31 * 128], BF16, kind="Internal").ap()

    with ExitStack() as es:
        fw = FW(nc, es)
        T = fw.T
        x_tok = fw.sb("x_tok", [128, 8, D], F32)
        xs_tok = fw.sb("xs_tok", [128, D], F32)
        hT = fw.sb("hT", [128, KC, TC], BF16)
        m = fw.sb("m", [128, KC, TC], BF16)
        wslot = fw.sb("wslot", [128, 3, KC, 512], BF16)
        vec = fw.sb("vec", [128, 2, 8, NV], F32)
        fng = fw.sb("fng", [128, D], F32)
        con = fw.sb("con", [128, NCON], F32)
        ident_b = fw.sb("ident_b", [128, 128], BF16)
        ones_b = fw.sb("ones_b", [128, 128], BF16)
        lbc = fw.sb("lbc", [128, 2, 8], F32)
        utail = fw.sb("utail", [128, 2, 8, 30], BF16)
        S_f = fw.sb("S_f", [128, 2, 8, 128], F32)
        chtail = fw.sb("chtail", [128, 2, 8, 2], F32)
        ss = fw.sb("ss", [128, 16], F32)
        lnv_n = fw.sb("lnv_n", [128, 16], F32)
        rstd_n = fw.sb("rstd_n", [128, 16], F32)
        ident_f = con[:, K_ID:K_ID + 128]
        maskpos = con[:, K_MP:K_MP + 128]
        maskneg = con[:, K_MN:K_MN + 128]
        mask01 = con[:, K_M1:K_M1 + 128]
        maskS = con[0:64, K_MS:K_MS + 64]
        rstP = con[:, K_RP:K_RP + 512]
        rstS = con[:, K_RS:K_RS + 64]
        seqm = con[0:64, K_SM:K_SM + 16]

        PA = fw.ps("PA", [128, 4, 512], F32)
        PT = fw.ps("PT", [128, 2, 512], F32)
        PC = fw.ps("PC", [128, 512], F32)
        PD = fw.ps("PD", [128, 512], F32)
        PTb = PT[:, 0, :].bitcast(BF16)
        pa_i = [0]

        def pa_next():
            i = pa_i[0]
            pa_i[0] = (i + 1) % 4
            return PA[:, i, :], T("PA", i)

        fw.dma("sync", con[:], cons, writes=[T("con")])
        fw.dma("sync", vec[:].rearrange("p a b c -> p (a b c)"), vecs, writes=[T("vec")])
        fw.dma("sync", fng[:], fngb, writes=[T("fng")])
        fw.copy(ident_b[:], ident_f, [T("con")], [T("identb")])
        fw.op("vector", lambda e: e.memset(ones_b[:], 1.0), [], [T("onesb")])
        fw.op("vector", lambda e: e.memset(S_f[:], 0.0), [], [T("S_f", 0), T("S_f", 1)])
        fw.op("vector", lambda e: e.memset(chtail[:], 0.0), [], [T("chtail", 0), T("chtail", 1)])
        fw.op("vector", lambda e: e.memset(ss[:], 1.0), [], [T("ss")])
        fw.op("vector", lambda e: e.memset(PC[:], 0.0), [], [T("PC")])
        fw.op("vector", lambda e: e.memset(lbc[:, 0, :], 0.0), [], [T("lbc")])
        fw.tt(lbc[:, 1, :], vec[:, 1, :, V_LB1], vec[:, 1, :, V_LB0], ALU.subtract, [T("vec")], [T("lbc")])
        fw.act(lbc[:, 1, :], lbc[:, 1, :], AF.Sigmoid, [T("lbc")], [T("lbc")])

        specs = []
        for ps_ in range(2):
            for l in range(2):
                specs += unit_specs(l, W)
        ust = {"next_load": 0, "next_use": 0}

        def load_unit(i):
            kind, segs = specs[i]
            sl = i % 3
            c0 = 0
            for seg in segs:
                n = seg.shape[-1]
                fw.dma("gpsimd", wslot[:, sl, :, c0:c0 + n], seg.rearrange("(kc p) n -> p kc n", p=128),
                       writes=[T("wslot", sl)])
                c0 += n

        def next_unit(kind, ahead=2):
            i = ust["next_use"]
            assert specs[i][0] == kind, (specs[i][0], kind, i)
            while ust["next_load"] < min(len(specs), i + 1 + ahead):
                load_unit(ust["next_load"])
                ust["next_load"] += 1
            ust["next_use"] += 1
            sl = i % 3
            return wslot[:, sl], T("wslot", sl)

        def mm_fm(slot, slot_t, cc, rhs_buf, rhs_toks, col0, n, ps_ap, ps_tok):
            def f(e):
                for kc in range(KC):
                    ins = e.matmul(ps_ap[:, :n], lhsT=slot[:, kc, cc * 128:(cc + 1) * 128],
                                   rhs=rhs_buf[:, kc, col0:col0 + n], start=(kc == 0), stop=(kc == KC - 1))
                return ins
            fw.op("tensor", f, [slot_t] + rhs_toks, [ps_tok])

        for i in range(3):
            load_unit(i)
        ust["next_load"] = 3
        _ck[0] = 0
        for ps_ in (range(2) if not _KSTOP else [-1]):
            pass
        try:
          for ps_ in range(2):
              hasS = ps_ == 1
              tiles = [(0, 0, 512), (1, 512, 512)] + ([(2, 1024, 64)] if hasS else [])
              t128 = [(i, 128, i * 128) for i in range(8)] + ([(8, 64, 1024)] if hasS else [])

              def hT_toks(ti):
                  return [T("hT", 8)] if ti == 2 else [T("hT", 4 * ti + j) for j in range(4)]

              def ftoks(name, ti, cs):
                  return [T(name, ti, c) for c in cs]

              def xap(i, np_, c0=0, c1=D):
                  return xs_tok[0:np_, c0:c1] if i == 8 else x_tok[0:np_, i, c0:c1]

              if ps_ == 0:
                  for i in range(8):
                      r0 = ps_ * TPP + i * 128
                      fw.dma("sync", x_tok[:, i, :], xp[r0:r0 + 128, :], writes=[T("x", i)])
              if hasS:
                  fw.dma("sync", xs_tok[0:64, :], xs, writes=[T("x", 8)])

              for l in range(2):
                  vcol = lambda c, r: vec[:, l, c, r:r + 1]
                  with ExitStack() as bs:
                    if l == 0:
                      fw.es = bs
                      xn_ring = Ring(fw, "xn%d%d" % (ps_, l), [128, D], F32, 2)
                      nt128 = t128 if ps_ == 0 else [(8, 64, 1024)]
                      cs9 = slice(0, 9) if ps_ == 0 else slice(8, 9)
                      for (i, np_, c0) in nt128:
                          junk, jt = xn_ring.next()
                          fw.act(junk[0:np_, :], xap(i, np_), AF.Square, [T("x", i)], [jt, T("ss")],
                                 accum_out=ss[0:np_, i:i + 1])
                      fw.act(lnv_n[:, cs9], ss[:, cs9], AF.Ln, [T("ss")], [T("lnvn")], scale=1.0 / D, bias=EPS)
                      fw.act(rstd_n[:, cs9], lnv_n[:, cs9], AF.Exp, [T("lnvn")], [T("rstdn")], scale=-0.5)
                      for (i, np_, c0) in nt128:
                          xn, xt = xn_ring.next()
                          fw.ts(xn[0:np_, :], xap(i, np_), rstd_n[0:np_, i:i + 1], ALU.mult, [T("x", i), T("rstdn")], [xt])

                          def tr(e, xn=xn, np_=np_):
                              for kc in range(KC):
                                  ins = e.transpose(out=PT[:, kc // 4, (kc % 4) * 128:(kc % 4) * 128 + np_],
                                                    in_=xn[0:np_, kc * 128:(kc + 1) * 128],
                                                    identity=ident_f[0:np_, 0:np_])
                              return ins
                          fw.op("tensor", tr, [xt, T("con")], [T("PT", 0), T("PT", 1)])
                          for b in range(2):
                              fw.tt(hT[:, 4 * b:4 * b + 4, c0:c0 + np_],
                                    PT[:, b, :].rearrange("p (k t) -> p k t", k=4)[:, :, 0:np_],
                                    vec[:, l, 4 * b:4 * b + 4, V_NG:V_NG + 1].broadcast_to([128, 4, np_]),
                                    ALU.mult, [T("PT", b), T("vec")], [T("hT", i)])
                      fw.es = es
                  if l == 0:
                      fw.barrier()

                  ck("normdone")
                  with ExitStack() as bs:
                      fw.es = bs
                      tg = "A%d%d" % (ps_, l)
                      nl = NL if hasS else NLMAX
                      u_bf = fw.sb("u_bf" + tg, [128, 8, 30 + TPP], BF16)
                      za = fw.sb("za" + tg, [128, 8, TC], BF16)
                      ca = fw.sb("ca" + tg, [128, 8, nl], F32)
                      sig_ring = Ring(fw, "sig" + tg, [128, 512], F32, 3)
                      dg_ring = Ring(fw, "dg" + tg, [128, 31, 128], BF16, 2)
                      sq_ring = Ring(fw, "sq" + tg, [128, nl], BF16, 3)
                      cb_ring = Ring(fw, "cb" + tg, [128, nl], BF16, 3)
                      st_mu = fw.sb("stmu" + tg, [128, nl], F32)
                      st_a = fw.sb("sta" + tg, [128, nl], F32)
                      st_rs = fw.sb("strs" + tg, [128, nl], F32)
                      st_nm = fw.sb("stnm" + tg, [128, nl], F32)
                      sl_ring = Ring(fw, "sl" + tg, [128, nl], F32, 2)
                      def build_dg(lb_, cs_=range(8)):
                          for c in cs_:
                              dgb, dgt = dg_ring.next()
                              fw.op("gpsimd", lambda e, dgb=dgb, c=c: e.affine_select(
                                  out=dgb[:], in_=vec[:, lb_, c, V_CAW:V_CAW + 31].unsqueeze(2).broadcast_to([128, 31, 128]),
                                  pattern=[[0, 31], [1, 128]], compare_op=ALU.is_equal, fill=0.0, base=0,
                                  channel_multiplier=-1), [T("vec")], [dgt])
                              fw.dma("sync", dgscr[lb_, c], dgb[:].rearrange("p k j -> p (k j)"), reads=[dgt],
                                     writes=[T("dgscr", lb_, c)])
                      if ps_ == 0 and l == 0:
                          build_dg(0)
                      if False:
                          for c in range(8):
                              dgb, dgt = dg_ring.next()
                              fw.op("gpsimd", lambda e, dgb=dgb, c=c: e.affine_select(
                                  out=dgb[:], in_=vec[:, l, c, V_CAW:V_CAW + 31].unsqueeze(2).broadcast_to([128, 31, 128]),
                                  pattern=[[0, 31], [1, 128]], compare_op=ALU.is_equal, fill=0.0, base=0,
                                  channel_multiplier=-1), [T("vec")], [dgt])
                              fw.dma("sync", dgscr[l, c], dgb[:].rearrange("p k j -> p (k j)"), reads=[dgt],
                                     writes=[T("dgscr", l, c)])
                      if hasS:
                          us_bf = fw.sb("us_bf" + tg, [128, 8, 16, 34], BF16)
                          us_new = fw.sb("us_new" + tg, [128, 8, 64], F32)
                          up_tail = fw.sb("up_tail" + tg, [128, 8, 30], F32)
                          asm_ring = Ring(fw, "asm" + tg, [128, 2, 16, 30], F32, 2)
                          fw.dma("gpsimd", us_bf[:].rearrange("p c j t -> p c (j t)"), scaP[:, l],
                                 writes=[T("us_bf")], max_dma_last_dim=2048)
                          fw.copy(u_bf[:, :, 0:30], utail[:, l], [T("utail", l)], [T("u_hist")])
                      else:
                          fw.op("vector", lambda e: e.memset(u_bf[:, :, 0:30], 0.0), [], [T("u_hist")])
                      for cp in range(4):
                          slot, st = next_unit("AGV")
                          for (ti, col0, n) in tiles:
                              for j in range(2):
                                  c = 2 * cp + j
                                  pg, pgt = pa_next()
                                  mm_fm(slot, st, j, hT, hT_toks(ti), col0, n, pg, pgt)
                                  sg, sgt = sig_ring.next()
                                  fw.act(sg[:, :n], pg[:, :n], AF.Sigmoid, [pgt], [sgt])
                                  pv, pvt = pa_next()
                                  mm_fm(slot, st, 2 + j, hT, hT_toks(ti), col0, n, pv, pvt)
                                  if ti < 2:
                                      fw.tt(u_bf[:, c, 30 + col0:30 + col0 + n], pv[:, :n], sg[:, :n], ALU.mult,
                                            [pvt, sgt], [T("u", ti, c)])
                                      if hasS and ti == 1:
                                          fw.tt(up_tail[:, c, :], pv[:, 482:512], sg[:, 482:512], ALU.mult,
                                                [pvt, sgt], [T("up_tail")])
                                  else:
                                      fw.tt(us_new[:, c, :], pv[:, :64], sg[:, :64], ALU.mult, [pvt, sgt],
                                            [T("us_new", c)])
                                      fw.copy(us_bf[:, c, :, 30:34],
                                              us_new[:, c, :].rearrange("p (j t) -> p j t", t=4),
                                              [T("us_new", c), T("us_bf")], [T("us", c)], eng="gpsimd")
                      if not hasS:
                          fw.copy(utail[:, l], u_bf[:, :, TPP:TPP + 30], [T("u", 1, c) for c in range(8)],
                                  [T("utail", l)])
                      ck("AGVdone")
                      ltiles = [((q * nl) // 512, q * nl, nl) for q in range(TPP // nl)] + ([(2, 1024, 64)] if hasS else [])
                      pend = None
                      for (ti, col0, n) in ltiles:
                          for c in range(8):
                              dgb, dgt = dg_ring.next()
                              fw.dma("sync", dgb[:].rearrange("p k j -> p (k j)"), dgscr[l, c], reads=[T("dgscr", l, c)],
                                     writes=[dgt])
                              pc, pct = pa_next()
                              if ti < 2:
                                  rtoks = [T("u", ti, c), T("u_hist")] + ([T("u", ti - 1, c)] if ti > 0 else [])

                                  def cv(e, dgb=dgb, c=c, col0=col0, n=n, pc=pc):
                                      for k in range(31):
                                          ins = e.matmul(pc[:, :n], lhsT=dgb[:, k, :], rhs=u_bf[:, c, col0 + k:col0 + k + n],
                                                         start=(k == 0), stop=(k == 30))
                                      return ins
                              else:
                                  rtoks = [T("us", c), T("us_bf")]

                                  def cv(e, dgb=dgb, c=c, pc=pc):
                                      for k in range(31):
                                          ins = e.matmul(pc[:, :64], lhsT=dgb[:, k, :], rhs=us_bf[:, c, :, k:k + 4],
                                                         start=(k == 0), stop=(k == 30))
                                      return ins
                              fw.op("tensor", cv, [dgt] + rtoks, [pct])
                              bcol = vcol(c, V_CAB)
                              fw.act(ca[:, c, :n], pc[:, :n], AF.Identity, [pct, T("vec")], [T("ca", c)], bias=bcol)
                              sq, sqt = sq_ring.next()
                              fw.act(sq[:, :n], pc[:, :n], AF.Square, [pct, T("vec")], [sqt], bias=bcol)
                              cb, cbt = cb_ring.next()
                              fw.act(cb[:, :n], pc[:, :n], AF.Identity, [pct, T("vec")], [cbt], bias=bcol)

                              def stm(e, cb=cb, sq=sq, c=c, n=n):
                                  e.matmul(PC[:, :n], lhsT=ones_b[:], rhs=cb[:, :n], start=(c == 0), stop=(c == 7))
                                  return e.matmul(PD[:, :n], lhsT=ones_b[:], rhs=sq[:, :n], start=(c == 0), stop=(c == 7))
                              if pend is not None:
                                  fw.op("tensor", pend[0], pend[1], [T("PC"), T("PD")])
                              pend = (stm, [cbt, sqt, T("onesb")])
                          fw.op("tensor", pend[0], pend[1], [T("PC"), T("PD")])
                          pend = None
                          fw.act(st_mu[:, :n], PC[:, :n], AF.Copy, [T("PC")], [T("stmu")], scale=1.0 / D)
                          fw.tt(st_a[:, :n], st_mu[:, :n], st_mu[:, :n], ALU.mult, [T("stmu")], [T("sta")])
                          fw.stt(st_a[:, :n], PD[:, :n], 1.0 / D, st_a[:, :n], ALU.mult, ALU.subtract,
                                 [T("PD"), T("sta")], [T("sta")])
                          fw.act(st_a[:, :n], st_a[:, :n], AF.Ln, [T("sta")], [T("sta")], bias=EPS)
                          fw.act(st_rs[:, :n], st_a[:, :n], AF.Exp, [T("sta")], [T("strs")], scale=-0.5)
                          fw.stt(st_nm[:, :n], st_mu[:, :n], -1.0, st_rs[:, :n], ALU.mult, ALU.mult,
                                 [T("stmu"), T("strs")], [T("stnm")])
                          for c in range(8):
                              fw.tt(ca[:, c, :n], ca[:, c, :n], st_rs[:, :n], ALU.mult, [T("ca", c), T("strs")], [T("ca", c)])
                              fw.tt(ca[:, c, :n], ca[:, c, :n], st_nm[:, :n], ALU.add, [T("ca", c), T("stnm")], [T("ca", c)])
                              fw.act(za[:, c, col0:col0 + n], ca[:, c, :n], AF.Silu, [T("ca", c), T("vec")], [T("za", ti, c)],
                                     scale=vcol(c, V_LNG), bias=vcol(c, V_LNB))
                      ck("convdone")
                      for h in range(2):
                          slot, st = next_unit("ZA")
                          for (ti, col0, n) in tiles:
                              for j in range(4):
                                  c = 4 * h + j
                                  pz, pzt = pa_next()
                                  mm_fm(slot, st, j, hT, hT_toks(ti), col0, n, pz, pzt)
                                  zs, zst = sig_ring.next()
                                  fw.act(zs[:, :n], pz[:, :n], AF.Silu, [pzt], [zst])
                                  fw.tt(za[:, c, col0:col0 + n], zs[:, :n], za[:, c, col0:col0 + n], ALU.mult,
                                        [zst, T("za", ti, c)], [T("za", ti, c)])
                      for cp in range(4):
                          if ps_ == 0 and l == 0:
                              build_dg(1, range(2 * cp, 2 * cp + 2))
                          slot, st = next_unit("AOG")
                          for (ti, col0, n) in tiles:
                              for j in range(2):
                                  c = 2 * cp + j
                                  pg, pgt = pa_next()
                                  mm_fm(slot, st, j, hT, hT_toks(ti), col0, n, pg, pgt)
                                  sg, sgt = sig_ring.next()
                                  fw.act(sg[:, :n], pg[:, :n], AF.Sigmoid, [pgt, T("vec")], [sgt], bias=vcol(c, V_GB))
                                  py, pyt = pa_next()
                                  mm_fm(slot, st, 2 + j, za, ftoks("za", ti, range(8)), col0, n, py, pyt)
                                  fw.tt(m[:, c, col0:col0 + n], py[:, :n], sg[:, :n], ALU.mult, [pyt, sgt],
                                        [T("m", ti, c)])
                      ck("AOGdone")
                      if hasS:
                          fw.dma("sync", napT[:, l], up_tail[:], reads=[T("up_tail")])
                          for cp in range(4):
                              ab, abt = asm_ring.next()
                              fw.dma("sync", ab[:].rearrange("p c j t -> p c (j t)"), scaS[:, l, 2 * cp:2 * cp + 2, :],
                                     writes=[abt])
                              for j in range(2):
                                  c = 2 * cp + j
                                  fw.copy(ab[:, j, :, 26:30], us_new[:, c, :].rearrange("p (j t) -> p j t", t=4),
                                          [T("us_new", c), abt], [abt], eng="gpsimd")
                              fw.dma("sync", nasT[:, l, 2 * cp:2 * cp + 2, :], ab[:].rearrange("p c j t -> p c (j t)"),
                                     reads=[abt])
                      fw.es = es
                  fw.barrier()

                  ck("Adone")
                  with ExitStack() as bs:
                      fw.es = bs
                      tg = "C%d%d" % (ps_, l)
                      cc = fw.sb("cc" + tg, [128, 8, TC], BF16)
                      chs_ring = Ring(fw, "chs" + tg, [128, 514], F32, 2)
                      cg_ring = Ring(fw, "cg" + tg, [128, 512], F32, 2)
                      acc_ring = Ring(fw, "acc" + tg, [128, 512], F32, 2)
                      gsc_ring = Ring(fw, "gsc" + tg, [128, 512], F32, 2)
                      tmp_ring = Ring(fw, "tmpc" + tg, [128, 512], BF16, 2)
                      if hasS:
                          chS = fw.sb("chS" + tg, [128, 8, 16, 6], F32)
                          ncs_sb = fw.sb("ncs_sb" + tg, [128, 8, 16, 2], F32)
                          fw.dma("sync", chS[:].rearrange("p c j t -> p c (j t)"), sccP[:, l], writes=[T("chS")])
                      for cp in range(4):
                          slot, st = next_unit("CGH")
                          for (ti, col0, n) in tiles:
                              for j in range(2):
                                  c = 2 * cp + j
                                  w0, w1, w2 = vcol(c, V_CCW), vcol(c, V_CCW + 1), vcol(c, V_CCW + 2)
                                  pg, pgt = pa_next()
                                  mm_fm(slot, st, j, hT, hT_toks(ti), col0, n, pg, pgt)
                                  cg, cgt = cg_ring.next()
                                  fw.act(cg[:, :n], pg[:, :n], AF.Copy, [pgt], [cgt])
                                  ph, pht = pa_next()
                                  mm_fm(slot, st, 2 + j, hT, hT_toks(ti), col0, n, ph, pht)
                                  ac, act_ = acc_ring.next()
                                  if ti < 2:
                                      chs, cht = chs_ring.next()
                                      fw.act(chs[:, 0:2], chtail[:, l, c, :], AF.Copy, [T("chtail", l)], [cht])
                                      fw.tt(chs[:, 2:2 + n], ph[:, :n], cg[:, :n], ALU.mult, [pht, cgt, cht], [cht])
                                      fw.act(chtail[:, l, c, :], chs[:, n:n + 2], AF.Copy, [cht], [T("chtail", l)])
                                      fw.ts(ac[:, :n], chs[:, 0:n], w0, ALU.mult, [cht, T("vec")], [act_])
                                      fw.stt(ac[:, :n], chs[:, 1:n + 1], w1, ac[:, :n], ALU.mult, ALU.add,
                                             [cht, act_, T("vec")], [act_])
                                      fw.stt(cc[:, c, col0:col0 + n], chs[:, 2:n + 2], w2, ac[:, :n], ALU.mult, ALU.add,
                                             [cht, act_, T("vec")], [T("cc", ti, c)])
                                  else:
                                      v3 = lambda a: a.rearrange("p (j t) -> p j t", t=4)
                                      fw.tt(chS[:, c, :, 2:6], v3(ph[:, :64]), v3(cg[:, :64]), ALU.mult,
                                            [pht, cgt, T("chS")], [T("chSc", c)])
                                      fw.ts(v3(ac[:, :64]), chS[:, c, :, 0:4], w0, ALU.mult, [T("chSc", c), T("vec")], [act_])
                                      fw.stt(v3(ac[:, :64]), chS[:, c, :, 1:5], w1, v3(ac[:, :64]), ALU.mult, ALU.add,
                                             [T("chSc", c), act_, T("vec")], [act_])
                                      fw.stt(v3(cc[:, c, 1024:1088]), chS[:, c, :, 2:6], w2, v3(ac[:, :64]), ALU.mult,
                                             ALU.add, [T("chSc", c), act_, T("vec")], [T("cc", ti, c)])
                      if hasS:
                          fw.dma("sync", ncpT[:, l], chtail[:, l], reads=[T("chtail", l)])
                          fw.copy(ncs_sb[:], chS[:, :, :, 4:6], [T("chSc", c) for c in range(8)], [T("ncs_sb")])
                          fw.dma("sync", ncsT[:, l], ncs_sb[:].rearrange("p c j t -> p c (j t)"), reads=[T("ncs_sb")])
                      for cp in range(4):
                          slot, st = next_unit("BGZ")
                          for (ti, col0, n) in tiles:
                              for j in range(2):
                                  c = 2 * cp + j
                                  pz, pzt = pa_next()
                                  mm_fm(slot, st, 2 + j, hT, hT_toks(ti), col0, n, pz, pzt)
                                  zs, zst = cg_ring.next()
                                  fw.act(zs[:, :n], pz[:, :n], AF.Silu, [pzt], [zst])
                                  pb, pbt = pa_next()
                                  mm_fm(slot, st, j, hT, hT_toks(ti), col0, n, pb, pbt)
                                  t1, t1t = acc_ring.next()
                                  fw.tt(t1[:, :n], pb[:, :n], zs[:, :n], ALU.mult, [pbt, zst], [t1t])
                                  fw.tt(cc[:, c, col0:col0 + n], t1[:, :n], cc[:, c, col0:col0 + n], ALU.mult,
                                        [t1t, T("cc", ti, c)], [T("cc", ti, c)])
                      for cp in range(4):
                          slot, st = next_unit("COG")
                          for (ti, col0, n) in tiles:
                              for j in range(2):
                                  c = 2 * cp + j
                                  pg, pgt = pa_next()
                                  mm_fm(slot, st, j, hT, hT_toks(ti), col0, n, pg, pgt)
                                  sg, sgt = gsc_ring.next()
                                  fw.act(sg[:, :n], pg[:, :n], AF.Sigmoid, [pgt, T("vec")], [sgt], bias=vcol(c, V_GB + 2))
                                  py, pyt = pa_next()
                                  mm_fm(slot, st, 2 + j, cc, ftoks("cc", ti, range(8)), col0, n, py, pyt)
                                  tm, tmt = tmp_ring.next()
                                  fw.tt(tm[:, :n], py[:, :n], sg[:, :n], ALU.mult, [pyt, sgt], [tmt])
                                  fw.tt(m[:, c, col0:col0 + n], m[:, c, col0:col0 + n], tm[:, :n], ALU.add,
                                        [tmt, T("m", ti, c)], [T("m", ti, c)])
                      fw.es = es
                  fw.barrier()

                  ck("Cdone")
                  with ExitStack() as bs:
                      fw.es = bs
                      tg = "B%d%d" % (ps_, l)
                      zb = fw.sb("zb" + tg, [128, 8, TC], BF16)
                      qT = fw.sb("qT" + tg, [128, 4, TC], BF16)
                      kT = fw.sb("kT" + tg, [128, 4, TC], BF16)
                      vtok = fw.sb("vtok" + tg, [128, 9, 512], BF16)
                      ebmid = fw.sb("ebmid" + tg, [128, 4, 8], F32)
                      eend = fw.sb("eend" + tg, [128, 4, 8], F32)
                      fr = {nm: Ring(fw, nm + tg, [128, 512], F32, (1 if (hasS and nm in ("fe", "fb", "fen")) else 2)) for nm in ("fe", "fl1", "fl2", "fb", "feb", "fen")}
                      Sp_f = fw.sb("Spf" + tg, [128, 4, 128], F32)
                      Sp_b = fw.sb("Spb" + tg, [128, 4, 128], BF16)
                      St2 = fw.sb("St2" + tg, [128, 4, 128], F32)
                      att_ring = Ring(fw, "att" + tg, [128, 512], BF16, 2)
                      kt_ring = Ring(fw, "ktk" + tg, [128, 512], BF16, 2)
                      osq_ring = Ring(fw, "osq" + tg, [128, 512], BF16, 1 if hasS else 2)
                      osb_ring = Ring(fw, "osb" + tg, [128, 512], F32, 1 if hasS else 2)
                      rs_ring = Ring(fw, "rsb" + tg, [128, 512], F32, 1 if hasS else 2)
                      gt_ring = Ring(fw, "gtb" + tg, [128, 512], BF16, 1 if hasS else 2)
                      if hasS:
                          eend_s = fw.sb("eends" + tg, [128, 4, 16], F32)
                          qs_f = fw.sb("qsf" + tg, [128, 4, 64], F32)
                          ktm = fw.sb("ktm" + tg, [64, 16, 128], BF16)
                          S0_ring = Ring(fw, "S0" + tg, [128, 4, 128], F32, 3)
                          So_ring = Ring(fw, "So" + tg, [128, 4, 128], F32, 2)
                      sl_i = [0]
                      for hg in range(2):
                          def fq_iter(slot, st, hp, ti, col0, n, j):
                              hl = hp * 2 + j
                              h = hg * 4 + hl
                              pf, pft = pa_next()
                              mm_fm(slot, st, j, hT, hT_toks(ti), col0, n, pf, pft)
                              fe, fet = fr["fe"].next()
                              fw.act(fe[:, :n], pf[:, :n], AF.Exp, [pft], [fet], scale=-1.0)
                              l1, l1t = fr["fl1"].next()
                              fw.act(l1[:, :n], fe[:, :n], AF.Ln, [fet, T("lbc")], [l1t], scale=lbc[:, l, h:h + 1], bias=1.0)
                              l2, l2t = fr["fl2"].next()
                              fw.act(l2[:, :n], fe[:, :n], AF.Ln, [fet], [l2t], bias=1.0)
                              fw.tt(l1[:, :n], l1[:, :n], l2[:, :n], ALU.subtract, [l1t, l2t], [l1t])
                              fb, fbt = fr["fb"].next()
                              rst = rstP if ti < 2 else rstS
                              fw.op("vector", lambda e, fb=fb, rst=rst, l1=l1, n=n: e.tensor_tensor_scan(
                                  out=fb[:, :n], data0=rst[:, :n], data1=l1[:, :n], initial=0.0,
                                  op0=ALU.mult, op1=ALU.add), [l1t, T("con")], [fbt])
                              if ti < 2:
                                  fw.act(ebmid[:, hl, ti * 4:ti * 4 + 4], fb[:, 63:512:128], AF.Exp, [fbt],
                                         [T("ebmid", hl)])
                                  fb3 = fb[:, :].rearrange("p (a t) -> p a t", t=128)
                                  l23 = l2[:, :].rearrange("p (a t) -> p a t", t=128)
                                  fw.tt(l23, fb3, fb3[:, :, 63:64].broadcast_to([128, 4, 128]), ALU.subtract,
                                        [fbt, l2t], [l2t])
                                  rsrc, rsrct = l2, l2t
                              else:
                                  rsrc, rsrct = fb, fbt
                              eb, ebt = fr["feb"].next()
                              fw.act(eb[:, :n], rsrc[:, :n], AF.Exp, [rsrct], [ebt])
                              en, ent = fr["fen"].next()
                              fw.act(en[:, :n], rsrc[:, :n], AF.Exp, [rsrct], [ent], scale=-1.0)
                              if ti < 2:
                                  fw.act(eend[:, hl, ti * 4:ti * 4 + 4], eb[:, 127:512:128], AF.Copy, [ebt],
                                         [T("eend", hl)])
                              else:
                                  fw.act(eend_s[:, hl, :], eb[:, 3:64:4], AF.Copy, [ebt], [T("eends", hl)])
                              fw.act(l2[:, :n], l1[:, :n], AF.Exp, [l1t, l2t], [l2t])
                              fw.ts(l2[:, :n], l2[:, :n], -1.0, ALU.mult, [l2t], [l2t], s2=1.0, op1=ALU.add)
                              fw.tt(kT[:, hl, col0:col0 + n], l2[:, :n], en[:, :n], ALU.mult, [l2t, ent],
                                    [T("kT", ti, hl)])
                              pq, pqt = pa_next()
                              mm_fm(slot, st, 2 + j, hT, hT_toks(ti), col0, n, pq, pqt)
                              fw.tt(qT[:, hl, col0:col0 + n], pq[:, :n], eb[:, :n], ALU.mult, [pqt, ebt],
                                    [T("qT", ti, hl)])
                              if ti == 2:
                                  fw.tt(qs_f[:, hl, :], pq[:, :64], eb[:, :64], ALU.mult, [pqt, ebt],
                                        [T("qsf", hl)])

                          def v_iter(slot, st, i, np_, c0, pv, pvt):
                              def vm(e, pv=pv, np_=np_, c0=c0, slot=slot):
                                  for kc in range(KC):
                                      ins = e.matmul(pv[0:np_, :], lhsT=hT[:, kc, c0:c0 + np_], rhs=slot[:, kc, :],
                                                     start=(kc == 0), stop=(kc == KC - 1))
                                  return ins
                              fw.op("tensor", vm, [st, T("hT", i)], [pvt])
                              if i % 2 == 0:
                                  fw.act(vtok[0:np_, i, :], pv[0:np_, :], AF.Copy, [pvt], [T("vtok", i)])
                              else:
                                  fw.copy(vtok[0:np_, i, :], pv[0:np_, :], [pvt], [T("vtok", i)])

                          slot, st = next_unit("FQ")
                          for (ti, col0, n) in tiles:
                              for j in range(2):
                                  fq_iter(slot, st, 0, ti, col0, n, j)
                          slot, st = next_unit("FQ")
                          slotv, stv = next_unit("V", ahead=1)
                          fq_list = [(ti, col0, n, j) for (ti, col0, n) in tiles for j in range(2)]
                          v_list = list(t128)
                          per = -(-len(v_list) // len(fq_list))
                          vk = 0
                          for (ti, col0, n, j) in fq_list:
                              fq_iter(slot, st, 1, ti, col0, n, j)
                              for _ in range(per):
                                  if v_list:
                                      (i, np_, c0) = v_list.pop(0)
                                      bank = (PC, T("PC")) if vk % 2 == 0 else (PD, T("PD"))
                                      vk += 1
                                      v_iter(slotv, stv, i, np_, c0, bank[0], bank[1])
                          while v_list:
                              (i, np_, c0) = v_list.pop(0)
                              bank = (PC, T("PC")) if vk % 2 == 0 else (PD, T("PD"))
                              vk += 1
                              v_iter(slotv, stv, i, np_, c0, bank[0], bank[1])
                          ck("FQdone")
                          ck("Vdone")
                          slot, st = next_unit("ZB")
                          for (ti, col0, n) in tiles:
                              for j in range(4):
                                  h = hg * 4 + j
                                  pz, pzt = pa_next()
                                  mm_fm(slot, st, j, hT, hT_toks(ti), col0, n, pz, pzt)
                                  fw.act(zb[:, h, col0:col0 + n], pz[:, :n], AF.Silu, [pzt], [T("zb", ti, h)])

                          def onorm(hl, ti, col0, n):
                              h = hg * 4 + hl
                              po, pot = PA[:, hl, :], T("PA", hl)
                              osq, osqt = osq_ring.next()
                              fw.act(osq[:, :n], po[:, :n], AF.Square, [pot], [osqt])
                              osb, osbt = osb_ring.next()
                              fw.act(osb[:, :n], po[:, :n], AF.Copy, [pot], [osbt])
                              fw.op("tensor", lambda e: e.matmul(PT[:, 1, :n], lhsT=ones_b[:], rhs=osq[:, :n],
                                                                 start=True, stop=True), [osqt, T("onesb")], [T("PT", 1)])
                              rs, rst_ = rs_ring.next()
                              fw.act(rs[:, :n], PT[:, 1, :n], AF.Ln, [T("PT", 1)], [rst_], scale=1.0 / 128, bias=EPS)
                              fw.act(rs[:, :n], rs[:, :n], AF.Exp, [rst_], [rst_], scale=-0.5)
                              fw.tt(osb[:, :n], osb[:, :n], rs[:, :n], ALU.mult, [osbt, rst_], [osbt])
                              fw.stt(zb[:, h, col0:col0 + n], osb[:, :n], vec[:, l, h, V_HG:V_HG + 1], zb[:, h, col0:col0 + n],
                                     ALU.mult, ALU.mult, [osbt, T("zb", ti, h), T("vec")], [T("zb", ti, h)])

                          ck("ZBdone")
                          H4 = range(4)
                          sft4 = [T("S_f", l, hg * 4 + q) for q in H4] + [T("S_f", l)]
                          bc = lambda a, j: a[:, :, j:j + 1].broadcast_to([128, 4, 128])
                          mp4 = maskpos.unsqueeze(1).broadcast_to([128, 4, 128])
                          mn4 = maskneg.unsqueeze(1).broadcast_to([128, 4, 128])
                          m14 = mask01.unsqueeze(1).broadcast_to([128, 4, 128])
                          ebt4 = [T("ebmid", q) for q in H4]
                          eet4 = [T("eend", q) for q in H4]
                          fw.tt(Sp_f[:], S_f[:, l, hg * 4:hg * 4 + 4, :], bc(ebmid, 0), ALU.mult, sft4 + ebt4, [T("Spf")])
                          fw.copy(Sp_b[:], Sp_f[:], [T("Spf")], [T("Spb")])
                          for i in range(8):
                              c0 = i * 128
                              ti = i // 4
                              kq = [T("kT", ti, q) for q in H4] + [T("qT", ti, q) for q in H4]

                              def att4(e, c0=c0):
                                  for q in H4:
                                      e.matmul(PC[0:64, q * 128:q * 128 + 128], lhsT=kT[:, q, c0:c0 + 64],
                                               rhs=qT[:, q, c0:c0 + 128], start=True, stop=True)
                                      ins = e.matmul(PC[64:128, q * 128 + 64:q * 128 + 128], lhsT=kT[:, q, c0 + 64:c0 + 128],
                                                     rhs=qT[:, q, c0 + 64:c0 + 128], start=True, stop=True)
                                  return ins
                              fw.op("tensor", att4, kq, [T("PC")])
                              at, att_ = att_ring.next()
                              fw.tt(at[:].rearrange("p (a t) -> p a t", t=128), PC[:].rearrange("p (a t) -> p a t", t=128),
                                    m14, ALU.mult, [T("PC"), T("con")], [att_])

                              def tr4(e, c0=c0):
                                  for q in H4:
                                      ins = e.transpose(out=PTb[:, q * 128:q * 128 + 128], in_=kT[:, q, c0:c0 + 128],
                                                        identity=ident_b[:])
                                  return ins
                              fw.op("tensor", tr4, kq + [T("identb")], [T("PT", 0)])
                              kt, ktt = kt_ring.next()
                              fw.act(kt[:], PTb[:, 0:512], AF.Copy, [T("PT", 0)], [ktt])
                              def kv4(e, i=i, kt=kt):
                                  for q in H4:
                                      ins = e.matmul(PD[:, q * 128:q * 128 + 128], lhsT=kt[:, q * 128:q * 128 + 128],
                                                     rhs=vtok[:, i, q * 128:q * 128 + 128], start=True, stop=True)
                                  return ins
                              fw.op("tensor", kv4, [ktt, T("vtok", i)], [T("PD")])
                              oc = slice((i % 4) * 128, (i % 4) * 128 + 128)

                              def om4(e, i=i, at=at, c0=c0, oc=oc):
                                  for q in H4:
                                      e.matmul(PA[:, q, oc], lhsT=vtok[:, i, q * 128:q * 128 + 128],
                                               rhs=at[:, q * 128:q * 128 + 128], start=True, stop=False)
                                      ins = e.matmul(PA[:, q, oc], lhsT=Sp_b[:, q, :], rhs=qT[:, q, c0:c0 + 128],
                                                     start=False, stop=True)
                                  return ins
                              fw.op("tensor", om4, [T("vtok", i), att_, T("Spb")] + kq, [T("PA", q) for q in H4])

                              PD3 = PD[:].rearrange("p (a t) -> p a t", t=128)
                              fw.tt(St2[:], Sp_f[:], PD3, ALU.add, [T("Spf"), T("PD")], [T("St2")])
                              if i < 7:
                                  fw.tt(St2[:], St2[:], bc(eend, i), ALU.mult, [T("St2")] + eet4, [T("St2")])
                                  fw.tt(Sp_b[:], St2[:], bc(ebmid, i + 1), ALU.mult, [T("St2")] + ebt4, [T("Spb")])
                                  fw.tt(Sp_f[:], St2[:], bc(ebmid, i + 1), ALU.mult, [T("St2")] + ebt4, [T("Spf")])
                              else:
                                  fw.tt(S_f[:, l, hg * 4:hg * 4 + 4, :], St2[:], bc(eend, i), ALU.mult, [T("St2")] + eet4,
                                        [T("S_f", l, hg * 4 + q) for q in H4])
                              if i % 4 == 3:
                                  for hl in range(4):
                                      onorm(hl, ti, ti * 512, 512)
                          ck("recdone")
                          if hasS and 'S' not in _KSKIP:
                              items = [(hl, jb) for hl in range(4) for jb in range(4)]
                              loaded = {}

                              def s0_load(idx):
                                  hl_, jb_ = items[idx]
                                  s0, s0t = S0_ring.next()
                                  fw.dma("sync", s0[:], shg[l, jb_ * 4:jb_ * 4 + 4, hg * 4 + hl_].rearrange("j k v -> k j v"),
                                         writes=[s0t])
                                  loaded[idx] = (s0, s0t)
                              PF = 2
                              for idx in range(min(PF, len(items))):
                                  s0_load(idx)
                              for hl in range(4):
                                  h = hg * 4 + hl
                                  fw.op("tensor", lambda e, hl=hl: e.matmul(
                                      PC[0:64, 0:64], lhsT=kT[:, hl, 1024:1088], rhs=qT[:, hl, 1024:1088], start=True, stop=True),
                                      [T("kT", 2, hl), T("qT", 2, hl)], [T("PC")])
                                  at, att_ = att_ring.next()
                                  fw.tt(at[0:64, 0:64], PC[0:64, 0:64], maskS, ALU.mult, [T("PC"), T("con")], [att_])
                                  fw.op("tensor", lambda e, hl=hl: e.transpose(
                                      out=PTb[0:64, 0:128], in_=kT[:, hl, 1024:1088], identity=ident_b[:]),
                                      [T("kT", 2, hl), T("identb")], [T("PT", 0)])
                                  kt, ktt = kt_ring.next()
                                  fw.act(kt[0:64, 0:128], PTb[0:64, 0:128], AF.Copy, [T("PT", 0)], [ktt])
                                  fw.tt(ktm[:, :, :], kt[0:64, 0:128].unsqueeze(1).broadcast_to([64, 16, 128]),
                                        seqm.unsqueeze(2).broadcast_to([64, 16, 128]), ALU.mult, [ktt, T("con")],
                                        [T("ktm")])
                                  fw.op("tensor", lambda e, hl=hl, at=at: e.matmul(
                                      PA[:, hl, 0:64], lhsT=vtok[0:64, 8, hl * 128:hl * 128 + 128], rhs=at[0:64, 0:64],
                                      start=True, stop=False), [T("vtok", 8), att_], [T("PA", hl)])
                                  for jb in range(4):
                                      idx = hl * 4 + jb
                                      if idx + PF < len(items):
                                          s0_load(idx + PF)
                                      s0, s0t = loaded.pop(idx)
                                      so, sot = So_ring.next()
                                      def sm(e, hl=hl, jb=jb, s0=s0):
                                          for jj in range(4):
                                              j = jb * 4 + jj
                                              ins = e.matmul(PA[:, hl, 4 * j:4 * j + 4], lhsT=s0[:, jj, :],
                                                             rhs=qs_f[:, hl, 4 * j:4 * j + 4], start=False, stop=(j == 15))
                                          return ins
                                      fw.op("tensor", sm, [s0t, T("qsf", hl)], [T("PA", hl)])

                                      def kvs(e, hl=hl, jb=jb):
                                          for jj in range(4):
                                              j = jb * 4 + jj
                                              ins = e.matmul(PD[:, jj * 128:jj * 128 + 128], lhsT=ktm[:, j, :],
                                                             rhs=vtok[0:64, 8, hl * 128:hl * 128 + 128], start=True, stop=True)
                                          return ins
                                      fw.op("tensor", kvs, [T("ktm"), T("vtok", 8)], [T("PD")])
                                      fw.tt(St2[:], s0[:], PD[:].rearrange("p (a t) -> p a t", t=128), ALU.add,
                                            [s0t, T("PD")], [T("St2")])
                                      fw.tt(so[:], St2[:], eend_s[:, hl, jb * 4:jb * 4 + 4].unsqueeze(2).broadcast_to([128, 4, 128]),
                                            ALU.mult, [T("St2"), T("eends", hl)], [sot])
                                      fw.dma("sync", nhs[l, jb * 4:jb * 4 + 4, h].rearrange("j k v -> k j v"), so[:],
                                             reads=[sot])
                                  onorm(hl, 2, 1024, 64)
                      if hasS:
                          fw.dma("sync", nhp[l].rearrange("h k v -> k h v"), S_f[:, l], reads=[T("S_f", l, h) for h in range(8)])
                      ck("Sdone")
                      for cp in range(4):
                          slot, st = next_unit("BOG")
                          for (ti, col0, n) in tiles:
                              for j in range(2):
                                  c = 2 * cp + j
                                  pg, pgt = pa_next()
                                  mm_fm(slot, st, j, hT, hT_toks(ti), col0, n, pg, pgt)
                                  sg, sgt = rs_ring.next()
                                  fw.act(sg[:, :n], pg[:, :n], AF.Sigmoid, [pgt, T("vec")], [sgt], bias=vcol(c, V_GB + 1))
                                  py, pyt = pa_next()
                                  mm_fm(slot, st, 2 + j, zb, ftoks("zb", ti, range(8)), col0, n, py, pyt)
                                  tm, tmt = gt_ring.next()
                                  fw.tt(tm[:, :n], py[:, :n], sg[:, :n], ALU.mult, [pyt, sgt], [tmt])
                                  fw.tt(m[:, c, col0:col0 + n], m[:, c, col0:col0 + n], tm[:, :n], ALU.add,
                                        [tmt, T("m", ti, c)], [T("m", ti, c)])
                      fw.es = es
                  fw.barrier()

                  ck("Bdone")
                  with ExitStack() as bs:
                      fw.es = bs
                      xn_ring = Ring(fw, "xo%d%d" % (ps_, l), [128, D], F32, 3)
                      yo_ring = Ring(fw, "yo%d%d" % (ps_, l), [128, D], F32, 2) if l == 1 else None
                      pend_tr = []

                      def flush_tr(keep, gl=1):
                          while len(pend_tr) > keep:
                              (xn, xt, i, np_, c0) = pend_tr.pop(0)

                              def tr(e, xn=xn, np_=np_):
                                  for kc in range(KC):
                                      ins = e.transpose(out=PT[:, kc // 4, (kc % 4) * 128:(kc % 4) * 128 + np_],
                                                        in_=xn[0:np_, kc * 128:(kc + 1) * 128],
                                                        identity=ident_f[0:np_, 0:np_])
                                  return ins
                              fw.op("tensor", tr, [xt, T("con")], [T("PT", 0), T("PT", 1)])
                              for b_ in range(2):
                                  fw.tt(hT[:, 4 * b_:4 * b_ + 4, c0:c0 + np_],
                                        PT[:, b_, :].rearrange("p (k t) -> p k t", k=4)[:, :, 0:np_],
                                        vec[:, gl, 4 * b_:4 * b_ + 4, V_NG:V_NG + 1].broadcast_to([128, 4, np_]),
                                        ALU.mult, [T("PT", b_), T("vec"), xt], [T("hT", i)])
                      pend_pre = []

                      def flush_pre(keep):
                          while len(pend_pre) > keep:
                              (i, c0) = pend_pre.pop(0)
                              xn2, xt2 = xn_ring.next()
                              fw.act(xn2[:, :], x_tok[:, i, :], AF.Square, [T("x", i)], [xt2, T("ss", i)],
                                     accum_out=ss[:, i:i + 1])
                              fw.act(lnv_n[:, i:i + 1], ss[:, i:i + 1], AF.Ln, [T("ss", i)], [T("lnvn", i)],
                                     scale=1.0 / D, bias=EPS)
                              fw.act(rstd_n[:, i:i + 1], lnv_n[:, i:i + 1], AF.Exp, [T("lnvn", i)],
                                     [T("rstdn", i)], scale=-0.5)
                              fw.ts(xn2[:, :], x_tok[:, i, :], rstd_n[:, i:i + 1], ALU.mult,
                                    [T("x", i), T("rstdn", i)], [xt2])
                              pend_tr.append((xn2, xt2, i, 128, c0))
                              flush_tr(2, 0)
                      for hf in range(2):
                          slot, st = next_unit("WO")
                          for (i, np_, c0) in t128:
                              po, pot = pa_next()
                              ti = 2 if i == 8 else i // 4

                              def wm(e, po=po, np_=np_, c0=c0, slot=slot):
                                  for kc in range(KC):
                                      ins = e.matmul(po[0:np_, :], lhsT=m[:, kc, c0:c0 + np_], rhs=slot[:, kc, :],
                                                     start=(kc == 0), stop=(kc == KC - 1))
                                  return ins
                              fw.op("tensor", wm, [st] + [T("m", ti, c) for c in range(8)], [pot])
                              xa = xap(i, np_, hf * 512, hf * 512 + 512)
                              fw.tt(xa, po[0:np_, :], xa, ALU.add, [pot, T("x", i)], [T("x", i)])
                              if hf == 1:
                                  xn, xt = (xn_ring if l == 0 else yo_ring).next()
                                  fw.act(xn[0:np_, :], xap(i, np_), AF.Square, [T("x", i)], [xt, T("ss", i)],
                                         accum_out=ss[0:np_, i:i + 1])
                                  fw.act(lnv_n[0:np_, i:i + 1], ss[0:np_, i:i + 1], AF.Ln, [T("ss", i)], [T("lnvn", i)],
                                         scale=1.0 / D, bias=EPS)
                                  fw.act(rstd_n[0:np_, i:i + 1], lnv_n[0:np_, i:i + 1], AF.Exp, [T("lnvn", i)],
                                         [T("rstdn", i)], scale=-0.5)
                                  if l == 0:
                                      fw.ts(xn[0:np_, :], xap(i, np_), rstd_n[0:np_, i:i + 1], ALU.mult,
                                            [T("x", i), T("rstdn", i)], [xt])
                                      pend_tr.append((xn, xt, i, np_, c0))
                                      flush_tr(2)
                                  else:
                                      fw.stt(xn[0:np_, :], xap(i, np_), rstd_n[0:np_, i:i + 1], fng[0:np_, :], ALU.mult,
                                             ALU.mult, [T("x", i), T("rstdn", i), T("fng")], [xt])
                                      if i == 8:
                                          fw.dma("sync", ys, xn[0:64, :], reads=[xt])
                                      else:
                                          r0 = ps_ * TPP + i * 128
                                          fw.dma("sync", yp[r0:r0 + 128, :], xn[:], reads=[xt])
                                      if ps_ == 0:
                                          r1 = (ps_ + 1) * TPP + i * 128
                                          fw.dma("sync", x_tok[:, i, :], xp[r1:r1 + 128, :], writes=[T("x", i)])
                                          pend_pre.append((i, c0))
                                          flush_pre(3)
                      flush_pre(0)
                      flush_tr(0, 1 if l == 0 else 0)
                      fw.es = es
                  fw.barrier()
              ck("Odone")
        except StopBuild:
            fw.es = es
        assert _KSTOP or ust["next_use"] == len(specs)
        fw.finish_all()
        print("[build] ops=%d waits=%d" % (fw.nops, fw.nwaits))
    return nc


def _consts():
    c = np.zeros((128, NCON), np.float32)
    c[:, K_ID:K_ID + 128] = np.eye(128, dtype=np.float32)
    s = np.arange(128)[:, None]
    t = np.arange(128)[None, :]
    keep = (s <= t)
    c[:, K_MP:K_MP + 128] = np.where(keep, BIG, 0.0)
    c[:, K_MN:K_MN + 128] = np.where(keep, -BIG, 0.0)
    c[:, K_M1:K_M1 + 128] = np.where(keep, 1.0, 0.0)
    s6 = np.arange(64)[:, None]
    t6 = np.arange(64)[None, :]
    c[0:64, K_MS:K_MS + 64] = ((s6 <= t6) & (s6 // 4 == t6 // 4)).astype(np.float32)
    c[:, K_RP:K_RP + 512] = (np.arange(512) % 128 != 0).astype(np.float32)[None, :]
    c[:, K_RS:K_RS + 64] = (np.arange(64) % 4 != 0).astype(np.float32)[None, :]
    c[0:64, K_SM:K_SM + 16] = (np.arange(64)[:, None] // 4 == np.arange(16)[None, :]).astype(np.float32)
    return c


_NC_CACHE = {}


def kernel(x_prompt, x_sample, state_conv_a, state_hgrn, state_conv_c, norm_g, w_in, gate_b, conv_a_w,
           conv_a_b, ln_g, ln_b, w_a_out, lb_raw, hg_norm_g, w_b_out, conv_c_w, w_c_out, w_o, final_norm_g):
    f = lambda a: np.ascontiguousarray(np.asarray(a, dtype=np.float32))
    x_prompt, x_sample, state_conv_a, state_hgrn, state_conv_c = map(f, (x_prompt, x_sample, state_conv_a,
                                                                        state_hgrn, state_conv_c))
    vl = []
    for l in range(2):
        rows = [f(lb_raw)[0], f(lb_raw)[1], f(norm_g)[l]] + list(f(gate_b)[l].reshape(3, D)) + \
               [f(conv_a_b)[l], f(ln_g)[l], f(ln_b)[l], f(hg_norm_g)[l]] + list(f(conv_c_w)[l]) + list(f(conv_a_w)[l])
        r = np.stack(rows)
        assert r.shape[0] == NV
        vl.append(r.reshape(NV, 8, 128).transpose(2, 1, 0))
    vecs = f(np.stack(vl, axis=1).reshape(128, 2 * 8 * NV))
    fngb = f(np.broadcast_to(f(final_norm_g)[None, :], (128, D)))
    cons = _consts()
    wts = {"w_in": f(w_in), "w_a_out": f(w_a_out), "w_b_out": f(w_b_out), "w_c_out": f(w_c_out), "w_o": f(w_o)}
    in_maps = []
    for i in range(8):
        sl = slice(16 * i, 16 * i + 16)
        sca = state_conv_a[:, sl].reshape(2, 16, 30, 8, 128).transpose(4, 0, 3, 1, 2)
        scaP = np.zeros((128, 2, 8, 16, 34), np.float32)
        scaP[..., 0:30] = sca
        scaS = np.zeros((128, 2, 8, 16, 30), np.float32)
        scaS[..., 0:26] = sca[..., 4:30]
        scc = state_conv_c[:, sl].reshape(2, 16, 2, 8, 128).transpose(4, 0, 3, 1, 2)
        sccP = np.zeros((128, 2, 8, 16, 6), np.float32)
        sccP[..., 0:2] = scc
        d = {"xp": f(x_prompt[i]), "xs": f(x_sample[sl].reshape(TS, D)),
             "scaP": f(scaP.reshape(128, 2, 8, 16 * 34)), "scaS": f(scaS.reshape(128, 2, 8, 16 * 30)),
             "shg": f(state_hgrn[:, sl]), "sccP": f(sccP.reshape(128, 2, 8, 96)),
             "vecs": vecs, "fngb": fngb, "cons": cons}
        d.update(wts)
        in_maps.append(d)
    if "nc" not in _NC_CACHE:
        _NC_CACHE["nc"] = build_program()
    res = run_bass_kernel_spmd(_NC_CACHE["nc"], in_maps, core_ids=list(range(8)))
    R = res.results
    y_prompt = np.stack([R[i]["yp"] for i in range(8)]).astype(np.float32)
    y_sample = np.concatenate([R[i]["ys"].reshape(16, 4, D) for i in range(8)], axis=0).astype(np.float32)
    nap = np.stack([R[i]["napT"].transpose(1, 3, 2, 0).reshape(2, 30, D) for i in range(8)], axis=1)
    nhp_ = np.stack([R[i]["nhp"] for i in range(8)], axis=1)
    ncp = np.stack([R[i]["ncpT"].transpose(1, 3, 2, 0).reshape(2, 2, D) for i in range(8)], axis=1)
    nas = np.concatenate([R[i]["nasT"].reshape(128, 2, 8, 16, 30).transpose(1, 3, 4, 2, 0).reshape(2, 16, 30, D)
                          for i in range(8)], axis=1)
    nhs_ = np.concatenate([R[i]["nhs"] for i in range(8)], axis=1)
    ncs = np.concatenate([R[i]["ncsT"].reshape(128, 2, 8, 16, 2).transpose(1, 3, 4, 2, 0).reshape(2, 16, 2, D)
                          for i in range(8)], axis=1)
    c = lambda a: np.ascontiguousarray(a, dtype=np.float32)
    return (c(y_prompt), c(y_sample), c(nap), c(nhp_), c(ncp), c(nas), c(nhs_), c(ncs))
```
